# Optimizing a Trainium2 kernel written in Bass

```python
import jax, jax.numpy as jnp
from jax import lax
import numpy as np

D_MODEL = 1024
BATCH = 32
SEQ = 2048
DEPTH = 2

CHUNK = 128
D_MIX = D_MODEL
GA_GROUPS = 4
GA_WIDTH = D_MIX // 4
GA_HEAD = GA_WIDTH // GA_GROUPS
MB_HEADS = 4
MB_WIDTH = 3 * D_MIX // 8
MB_HEAD = MB_WIDTH // MB_HEADS
MB_CONV = 4
SC_HEADS = 6
SC_WIDTH = D_MIX - GA_WIDTH - MB_WIDTH
SC_HEAD = SC_WIDTH // SC_HEADS
EPS = 1e-6
IN_SPLITS = (GA_WIDTH,) * 3 + (MB_WIDTH,) * 5 + (MB_HEADS,) * 2 + (SC_WIDTH,) * 4
D_IN = sum(IN_SPLITS)

kernel_name = "hybrid_gmlp_mlstm_stickbreaking_block"


def rms_norm(x, g):
    xf = x.astype(jnp.float32)
    y = xf * lax.rsqrt(jnp.mean(xf * xf, axis=-1, keepdims=True) + EPS)
    return (y * g.astype(jnp.float32)).astype(x.dtype)


def causal_conv(x, w, b):
    K, S = w.shape[0], x.shape[1]
    xp = jnp.pad(x, ((0, 0), (K - 1, 0), (0, 0)))
    y = b
    for tap in range(K):
        y = y + xp[:, tap:tap + S] * w[tap]
    return y


def spatial_gating_branch(u, v, v_norm_g, w_s, b_s):
    B, S, _ = u.shape
    nc = S // CHUNK
    u = jax.nn.gelu(u)
    v = rms_norm(jax.nn.gelu(v).reshape(B, nc, CHUNK, GA_GROUPS, GA_HEAD), v_norm_g)
    causal = jnp.tril(jnp.ones((CHUNK, CHUNK), dtype=bool))
    w = jnp.where(causal, w_s, 0).astype(v.dtype)
    sp = jnp.einsum('gts,bcsgd->bctgd', w, v) + b_s.T[:, :, None].astype(v.dtype)
    return u * sp.reshape(B, S, GA_WIDTH)


def mlstm_branch(q, k, v, i_raw, f_raw):
    out_dtype = v.dtype
    q, k, v = (a.astype(jnp.float32) for a in (q, k, v))
    B, S, H, d = q.shape
    nc = S // CHUNK
    k = k * (d ** -0.5)
    i_log = i_raw.astype(jnp.float32)
    f_log = jax.nn.log_sigmoid(f_raw.astype(jnp.float32))

    def to_chunks(a):
        a = a.reshape((B, nc, CHUNK, H) + a.shape[3:])
        return jnp.moveaxis(a, (1, 3), (0, 2))

    xs = (to_chunks(q), to_chunks(k), to_chunks(v), to_chunks(i_log), to_chunks(f_log))
    causal = jnp.tril(jnp.ones((CHUNK, CHUNK), dtype=bool))

    def step(carry, inp):
        C, n, m = carry
        qb, kb, vb, ib, lfb = inp
        bcum = jnp.cumsum(lfb, axis=-1)
        d_log = jnp.where(causal, bcum[..., :, None] - bcum[..., None, :] + ib[..., None, :], -jnp.inf)
        inter = bcum + m[..., None]
        m_t = jnp.maximum(inter, jnp.max(d_log, axis=-1))
        w_intra = jnp.exp(d_log - m_t[..., None])
        w_inter = jnp.exp(inter - m_t)
        s = jnp.einsum('bhtd,bhsd->bhts', qb, kb) * w_intra
        num = jnp.einsum('bhts,bhsd->bhtd', s, vb) + w_inter[..., None] * jnp.einsum('bhed,bhtd->bhte', C, qb)
        den = jnp.sum(s, axis=-1) + w_inter * jnp.einsum('bhd,bhtd->bht', n, qb)
        h = num / jnp.maximum(jnp.abs(den), jnp.exp(-m_t))[..., None]
        b_tot = bcum[..., -1]
        log_ws = b_tot[..., None] - bcum + ib
        m_new = jnp.maximum(b_tot + m, jnp.max(log_ws, axis=-1))
        decay = jnp.exp(b_tot + m - m_new)
        ws = jnp.exp(log_ws - m_new[..., None])
        C = decay[..., None, None] * C + jnp.einsum('bhs,bhse,bhsd->bhed', ws, vb, kb)
        n = decay[..., None] * n + jnp.einsum('bhs,bhsd->bhd', ws, kb)
        return (C, n, m_new), h

    init = (jnp.zeros((B, H, d, d), jnp.float32), jnp.zeros((B, H, d), jnp.float32),
            jnp.zeros((B, H), jnp.float32))
    _, hs = lax.scan(step, init, xs)
    h = jnp.moveaxis(hs, (0, 2), (1, 3)).reshape(B, S, H, d)
    return h.astype(out_dtype)


def stick_breaking_branch(q, k, v):
    B, S, H, d = q.shape
    scale = d ** -0.5
    outs = []
    for blk in range(S // CHUNK):
        t0 = blk * CHUNK
        kv_len = t0 + CHUNK
        qb, kb, vb = q[:, t0:kv_len], k[:, :kv_len], v[:, :kv_len]
        z = jnp.einsum('bthd,bshd->bhts', qb, kb).astype(jnp.float32) * scale
        t_idx = t0 + jnp.arange(CHUNK)
        s_idx = jnp.arange(kv_len)
        strict = s_idx[None, :] < t_idx[:, None]
        log_not = jnp.where(strict, jax.nn.log_sigmoid(-z), 0.0)
        suffix = lax.cumsum(log_not, axis=3, reverse=True) - log_not
        a = jnp.where(strict, jnp.exp(jax.nn.log_sigmoid(z) + suffix), 0.0)
        outs.append(jnp.einsum('bhts,bshd->bthd', a.astype(vb.dtype), vb))
    return jnp.concatenate(outs, axis=1)


def hybrid_layer(x, c_act, norm_g, w_ada, b_ada, w_in, ga_v_norm, ga_ws, ga_bs,
                 mb_conv_w, mb_conv_b, mb_b_i, mb_b_f, mb_h_norm, sc_q_norm, sc_k_norm, w_out):
    B, S, _ = x.shape
    shift, scale, gate = jnp.split(c_act @ w_ada + b_ada, 3, axis=-1)
    h = rms_norm(x, norm_g) * (1 + scale[:, None]) + shift[:, None]
    proj = h @ w_in
    offsets = [int(o) for o in np.cumsum(IN_SPLITS)[:-1]]
    (ga_u, ga_v, ga_z, mb_q, mb_k, mb_v, mb_o, mb_z, mb_i, mb_f,
     sc_q, sc_k, sc_v, sc_z) = jnp.split(proj, offsets, axis=-1)

    y_a = spatial_gating_branch(ga_u, ga_v, ga_v_norm, ga_ws, ga_bs) * jax.nn.silu(ga_z)

    qk = jax.nn.silu(causal_conv(jnp.concatenate([mb_q, mb_k], axis=-1), mb_conv_w, mb_conv_b))
    q_b, k_b = jnp.split(qk, 2, axis=-1)
    heads_b = lambda a: a.reshape(B, S, MB_HEADS, MB_HEAD)
    h_b = mlstm_branch(heads_b(q_b), heads_b(k_b), heads_b(mb_v), mb_i + mb_b_i, mb_f + mb_b_f)
    h_b = rms_norm(h_b, mb_h_norm).reshape(B, S, MB_WIDTH)
    y_b = jax.nn.sigmoid(mb_o) * h_b * jax.nn.silu(mb_z)

    heads_c = lambda a: a.reshape(B, S, SC_HEADS, SC_HEAD)
    h_c = stick_breaking_branch(rms_norm(heads_c(sc_q), sc_q_norm), rms_norm(heads_c(sc_k), sc_k_norm),
                                heads_c(sc_v))
    y_c = h_c.reshape(B, S, SC_WIDTH) * jax.nn.silu(sc_z)

    y = jnp.concatenate([y_a, y_b, y_c], axis=-1) @ w_out
    return x + gate[:, None] * y


def setup_inputs(seed: int = 0) -> dict:
    key = jax.random.key(seed)
    ks = jax.random.split(key, 17)
    nrm = lambda k, shape: jax.random.normal(k, shape, jnp.float32)
    D = D_MODEL
    return {
        "x": nrm(ks[0], (BATCH, SEQ, D)),
        "c": nrm(ks[1], (BATCH, D)),
        "norm_g": 1.0 + 0.02 * nrm(ks[2], (DEPTH, D)),
        "w_ada": nrm(ks[3], (DEPTH, D, 3 * D)) * (0.5 * D ** -0.5),
        "b_ada": 0.01 * nrm(ks[4], (DEPTH, 3 * D)),
        "w_in": nrm(ks[5], (DEPTH, D, D_IN)) * D ** -0.5,
        "ga_v_norm": 1.0 + 0.02 * nrm(ks[6], (DEPTH, GA_GROUPS, GA_HEAD)),
        "ga_ws": nrm(ks[7], (DEPTH, GA_GROUPS, CHUNK, CHUNK)) * CHUNK ** -0.5,
        "ga_bs": 1.0 + 0.02 * nrm(ks[8], (DEPTH, GA_GROUPS, CHUNK)),
        "mb_conv_w": nrm(ks[9], (DEPTH, MB_CONV, 2 * MB_WIDTH)) * MB_CONV ** -0.5,
        "mb_conv_b": 0.01 * nrm(ks[10], (DEPTH, 2 * MB_WIDTH)),
        "mb_b_i": 0.1 * nrm(ks[11], (DEPTH, MB_HEADS)),
        "mb_b_f": jnp.linspace(3.0, 6.0, MB_HEADS, dtype=jnp.float32)[None, :] + 0.01 * nrm(ks[12], (DEPTH, MB_HEADS)),
        "mb_h_norm": 1.0 + 0.02 * nrm(ks[13], (DEPTH, MB_HEADS, MB_HEAD)),
        "sc_q_norm": 1.0 + 0.02 * nrm(ks[14], (DEPTH, SC_HEAD)),
        "sc_k_norm": 1.0 + 0.02 * nrm(ks[15], (DEPTH, SC_HEAD)),
        "w_out": nrm(ks[16], (DEPTH, D_MIX, D)) * D_MIX ** -0.5,
    }


def reference(x, c, norm_g, w_ada, b_ada, w_in, ga_v_norm, ga_ws, ga_bs, mb_conv_w, mb_conv_b,
              mb_b_i, mb_b_f, mb_h_norm, sc_q_norm, sc_k_norm, w_out):
    c_act = jax.nn.silu(c)
    for l in range(DEPTH):
        x = hybrid_layer(x, c_act, norm_g[l], w_ada[l], b_ada[l], w_in[l], ga_v_norm[l], ga_ws[l],
                         ga_bs[l], mb_conv_w[l], mb_conv_b[l], mb_b_i[l], mb_b_f[l], mb_h_norm[l],
                         sc_q_norm[l], sc_k_norm[l], w_out[l])
    return x
```

```python
import numpy as np
import concourse.bass as bass
import concourse.mybir as mybir
from concourse.bass_utils import run_bass_kernel_spmd

F32 = mybir.dt.float32
BF16 = mybir.dt.bfloat16
AF = mybir.ActivationFunctionType
ALU = mybir.AluOpType
AX = mybir.AxisListType

D = 1024
DIN = 4232
EPS = 1e-6
CS_INV = 4.0 * (96.0 ** 0.5)
BIG = 30000.0


class Buf:
    __slots__ = ("name", "w", "r", "excl")

    def __init__(self, name, excl=False):
        self.name = name
        self.w = {}
        self.r = {}
        self.excl = excl


class Sched:
    def __init__(self, nc):
        self.nc = nc
        self.eng = {"pe": nc.tensor, "act": nc.scalar, "dve": nc.vector, "pool": nc.gpsimd, "sp": nc.sync}
        self.sems = {}
        self.cnt = {}
        self.known = {e: {} for e in self.eng}
        self._ctx = []
        for e in ["pe", "act", "dve", "pool"]:
            self._mksem(e)
        self.ninst = 0
        self.nwaits = 0

    def _mksem(self, key):
        cm = self.nc.semaphore("s_" + key)
        s = cm.__enter__()
        self._ctx.append(cm)
        self.sems[key] = s
        self.cnt[key] = 0

    def _wait(self, e, k, v):
        if self.known[e].get(k, 0) >= v:
            return
        self.eng[e].wait_ge(self.sems[k], v)
        self.known[e][k] = v
        self.nwaits += 1

    def _deps(self, e, reads, writes):
        for b in reads:
            for k, v in b.w.items():
                self._wait(e, k, v)
        for b in writes:
            for k, v in b.w.items():
                if k != e:
                    self._wait(e, k, v)
            for k, v in b.r.items():
                if k != e:
                    self._wait(e, k, v)

    def _record(self, k, v, reads, writes):
        for b in reads:
            if b.r.get(k, 0) < v:
                b.r[k] = v
        for b in writes:
            b.w[k] = v
            b.r = {}

    @staticmethod
    def _flat(lst):
        out = []
        for b in lst:
            if isinstance(b, (list, tuple)):
                out.extend(b)
            else:
                out.append(b)
        return out

    def op(self, e, fn, reads=(), writes=()):
        reads, writes = self._flat(reads), self._flat(writes)
        ex = [b for b in reads if b.excl]
        if ex:
            reads = [b for b in reads if not b.excl]
            writes = list(writes) + ex
        self._deps(e, reads, writes)
        ins = fn()
        self.cnt[e] += 1
        ins.then_inc(self.sems[e], 1)
        self._record(e, self.cnt[e], reads, writes)
        self.ninst += 1

    def dma(self, q, key, out, in_, reads=(), writes=()):
        if key not in self.sems:
            self._mksem(key)
        reads, writes = self._flat(reads), self._flat(writes)
        self._deps(q, reads, writes)
        ins = self.eng[q].dma_start(out=out, in_=in_)
        self.cnt[key] += 16
        ins.then_inc(self.sems[key], 16)
        self._record(key, self.cnt[key], reads, writes)
        self.ninst += 1

    def barrier(self):
        for e in self.eng:
            for k, v in self.cnt.items():
                if v > 0:
                    self._wait(e, k, v)


class _Stop(Exception):
    pass


def build(NSEQ, S, DEPTH, debug=False):
    import os
    STOP = os.environ.get("KSTOP", "")
    NT = S // 128
    GT = 2
    GW = GT * 128
    NG = NT // GT
    nc = bass.Bass("TRN2", target_bir_lowering=False)
    sc = Sched(nc)
    V, A, G, T = nc.vector, nc.scalar, nc.gpsimd, nc.tensor

    def dram(name, shape, dt=F32, kind="ExternalInput"):
        return nc.dram_tensor(name, list(shape), dt, kind=kind).ap()

    x_d = dram("x", [NSEQ, S, D])
    cT_d = dram("cT", [128, 8, NSEQ])
    wada_d = dram("w_ada", [DEPTH, D, 3 * D])
    badaT_d = dram("b_adaT", [128, DEPTH, 24])
    normgT_d = dram("normgT", [128, DEPTH, 8])
    win_d = dram("w_in", [DEPTH, D, DIN])
    wout_d = dram("w_out", [DEPTH, D, D])
    gav_d = dram("gav", [DEPTH, 256])
    wsT_d = dram("wsT", [DEPTH, 128, 4, 128])
    bsT_d = dram("bsT", [128, DEPTH, 4])
    convw_d = dram("convw", [96, DEPTH, 8, 4])
    convb_d = dram("convb", [96, DEPTH, 8])
    gbias_d = dram("gbias", [DEPTH, 8])
    hnorm_d = dram("hnorm", [DEPTH, 384])
    qnorm_d = dram("qnorm", [DEPTH, 64])
    knorm_d = dram("knorm", [DEPTH, 64])
    consts_d = dram("consts", [128, 8, 128])
    y_d = dram("y", [NSEQ, S, D], kind="ExternalOutput")
    Bwinbf = [[Buf("winbf%d_%d" % (l, i)) for i in range(16)] for l in range(DEPTH)]
    Bwoutbf = [[Buf("woutbf%d_%d" % (l, i)) for i in range(8)] for l in range(DEPTH)]
    if debug:
        dbg_d = dram("dbg", [128, 8, GW], kind="ExternalOutput")

    def sb(name, shape, dt=F32):
        return nc.alloc_sbuf_tensor("sb_" + name, list(shape), dt)

    cF = sb("cF", [128, 4, 128]); BcF = Buf("cF")
    cB = sb("cB", [128, 4, 128], BF16); BcB = Buf("cB")
    identF, triU, pmaskI, onesF = cF[:, 0, :], cF[:, 1, :], cF[:, 2, :], cF[:, 3, :]
    identB, nTriL, nmaskS, nOnes = cB[:, 0, :], cB[:, 1, :], cB[:, 2, :], cB[:, 3, :]
    modT = sb("modT", [128, DEPTH, 24, NSEQ]); BmodT = Buf("modT")
    Amod = sb("Amod", [128, DEPTH, NSEQ, 8]); BAmod = Buf("Amod")
    normgT = sb("normgT", [128, DEPTH, 8]); Bnormg = Buf("normg")
    badaT = sb("badaT", [128, DEPTH, 24]); Bbada = Buf("bada")
    cact = sb("cact", [128, 8, NSEQ]); Bcact = Buf("cact")
    ctmp = sb("ctmp", [128, 8, NSEQ]); Bctmp = Buf("ctmp")

    sc.dma("sp", "c0", cF[:], consts_d[:, 0:4, :], writes=[BcF])
    sc.dma("sp", "c1", normgT[:], normgT_d[:, :, :], writes=[Bnormg])
    sc.dma("sp", "c2", badaT[:], badaT_d[:, :, :], writes=[Bbada])
    sc.dma("sp", "c3", cact[:], cT_d[:, :, :], writes=[Bcact])
    sc.op("act", lambda: A.activation(out=ctmp[:], in_=cact[:], func=AF.Tanh, scale=0.5), [Bcact], [Bctmp])
    sc.op("dve", lambda: V.scalar_tensor_tensor(out=ctmp[:], in0=ctmp[:], scalar=1.0, in1=cact[:], op0=ALU.add, op1=ALU.mult), [Bctmp, Bcact], [Bctmp])
    sc.op("dve", lambda: V.tensor_scalar_mul(out=cact[:], in0=ctmp[:], scalar1=0.5), [Bctmp], [Bcact])

    PB = [nc.alloc_psum_tensor("pb%d" % i, [128, 512], F32) for i in range(8)]
    PBb = [Buf("pb%d" % i, excl=True) for i in range(8)]
    PB4b = PB[4][:].bitcast(BF16)

    def pq(bank, c0, c1):
        return [PBb[bank]]

    xres = sb("xres", [128, NT, D]); Bx = [Buf("x%d" % t) for t in range(NT)]
    win = sb("win", [128, 8, DIN], BF16); Bwin = [Buf("win%d" % i) for i in range(8)]

    def load_win(l_):
        for kc_ in range(8):
            sc.dma("pool", "win", win[:, kc_, :], win_d[l_, kc_ * 128:(kc_ + 1) * 128, :], writes=[Bwin[kc_]])

    for t in range(NT):
        sc.dma("sp", "x%d" % t, xres[:, t, :], x_d[0, t * 128:(t + 1) * 128, :], writes=[Bx[t]])
    load_win(0)

    NSLOT = 3
    stg_cm = [nc.sbuf_tensor("stg%d" % i, [128, 8, 512], F32) for i in range(NSLOT)]
    stg = [cm.__enter__() for cm in stg_cm]
    Bstg = [Buf("stg%d" % i) for i in range(NSLOT)]
    cst_v = stg[0][:, 0:4, 0:128]
    sc.dma("sp", "stg0", cst_v, consts_d[:, 4:8, :], writes=[Bstg[0]])
    sc.op("dve", lambda: V.tensor_copy(out=cB[:], in_=cst_v), [Bstg[0]], [BcB])
    slot = 1
    for l in range(DEPTH):
        for pc in range(6):
            s = slot
            slot = (slot + 1) % NSLOT
            sv = stg[s]
            sc.dma("sp", "stg%d" % s, sv[:], wada_d[l, :, pc * 512:(pc + 1) * 512].rearrange("(k p) n -> p k n", p=128), writes=[Bstg[s]])
            for fc in range(4):
                j = pc * 4 + fc
                bank = j % 2
                for kc in range(8):
                    sc.op("pe", lambda: T.matmul(PB[bank][:, 0:NSEQ], lhsT=sv[:, kc, fc * 128:(fc + 1) * 128], rhs=cact[:, kc, :], start=(kc == 0), stop=(kc == 7)),
                          [Bstg[s], Bcact], pq(bank, 0, NSEQ))
                sc.op("dve", lambda: V.tensor_scalar_add(out=modT[:, l, j, :], in0=PB[bank][:, 0:NSEQ], scalar1=badaT[:, l, j:j + 1]),
                      pq(bank, 0, NSEQ) + [Bbada], [BmodT])
    for l in range(DEPTH):
        for b in range(NSEQ):
            sc.op("dve", lambda: V.scalar_tensor_tensor(out=Amod[:, l, b, :], in0=modT[:, l, 8:16, b], scalar=1.0, in1=normgT[:, l, :], op0=ALU.add, op1=ALU.mult),
                  [BmodT, Bnormg], [BAmod])
    sc.barrier()
    if STOP == "pro":
        return nc
    for cm in stg_cm[::-1]:
        cm.__exit__(None, None, None)

    wout = sb("wout", [128, 8, D], BF16); Bwout = Buf("wout")
    kT = sb("kT", [128, 3, S], BF16); BkT = [Buf("kT%d" % t) for t in range(NT)]
    vc = sb("vc", [128, NT, 384], BF16); Bvc = [Buf("vc%d" % t) for t in range(NT)]
    wsTb = sb("wsTb", [128, 4, 128], BF16); BwsTb = Buf("wsTb")
    gainA = sb("gainA", [128, 256]); BgainA = Buf("gainA")
    bsTs = sb("bsTs", [128, 4]); BbsT = Buf("bsT")
    convw = sb("convw", [96, 8, 4]); Bconvw = Buf("convw")
    convb = sb("convb", [96, 8]); Bconvb = Buf("convb")
    gbias = sb("gbias", [128, 8]); Bgbias = Buf("gbias")
    hnorm = sb("hnorm", [128, 384]); Bhnorm = Buf("hnorm")
    qnb = sb("qnb", [128, 64]); Bqnb = Buf("qnb")
    knb = sb("knb", [128, 64]); Bknb = Buf("knb")
    hT = sb("hT", [128, 8, GW], BF16); BhT = [Buf("hT%d" % r) for r in range(GT)]
    qkT = sb("qkT", [96, 8, GW], BF16); BqkT = [Buf("qkT%d" % u) for u in range(8)]
    hist = sb("hist", [96, 8, 3]); Bhist = [Buf("hist%d" % u) for u in range(8)]
    qTa = sb("qTa", [128, 3, GW], BF16); qTb = sb("qTb", [128, 3, GW], BF16)
    BqT = [Buf("qT%d" % r) for r in range(GT)]
    szc = sb("szc", [128, GT, 384], BF16); Bszc = [Buf("szc%d" % r) for r in range(GT)]
    ss = sb("ss", [128, 16]); Bss = Buf("ss")
    vaug = sb("vaug", [128, 4, 97], BF16); Bvaug = Buf("vaug")
    gt = sb("gt", [128, 8]); Bgt = Buf("gt")
    sm = sb("sm", [128, 48]); Bsm = Buf("sm")
    C32 = sb("C32", [96, 4, 97]); BC32 = [Buf("C32_%d" % h) for h in range(4)]
    Cb = sb("Cb", [96, 4, 97], BF16); BCb = [Buf("Cb_%d" % h) for h in range(4)]
    ARENA = 11328
    CH = 64
    arena = sb("arena", [128, ARENA // 4])
    arena_bf = arena[:].bitcast(BF16)
    Bar = [Buf("ar%d" % i) for i in range((ARENA + CH - 1) // CH)]

    def av(off, shape, dt=F32, parts=128):
        n = int(np.prod(shape[1:])) * (4 if dt == F32 else 2)
        assert off % 64 == 0 and off + n <= ARENA
        bufs = Bar[off // CH:(off + n + CH - 1) // CH]
        if dt == F32:
            ap = arena[0:parts, off // 4:(off + n) // 4]
        else:
            ap = arena_bf[0:parts, off // 2:(off + n) // 2]
        if len(shape) == 3:
            ap = ap.rearrange("p (a b) -> p a b", a=shape[1])
        return ap, bufs

    xn, Bxn = av(0, [128, D])
    raw0, Braw0 = av(0, [96, GW + 3], parts=96)
    raw1, Braw1 = av(1088, [96, GW + 3], parts=96)
    raw = [raw0, raw1]; Braw = [Braw0, Braw1]
    accs = [None] * 2; Baccs = [None] * 2; thqs = [None] * 2; Bthqs = [None] * 2
    for i_ in range(2):
        accs[i_], Baccs[i_] = av(2176 + i_ * 2048, [96, GW], parts=96)
        thqs[i_], Bthqs[i_] = av(3200 + i_ * 2048, [96, GW], parts=96)
    tmpP, BtmpP = av(6272, [96, GW], parts=96)
    wsTf, BwsTf = av(0, [128, 4, 128])
    guv_, Bguv_ = av(0, [128, 512]); zgA_, BzgA_ = av(2048, [128, 256]); tmpA_, BtmpA_ = av(3072, [128, 256])
    w512 = [guv_, zgA_, tmpA_]; Bw512 = [Bguv_, BzgA_, BtmpA_]
    w384 = [None] * 3; Bw384 = [None] * 3
    for i_, o_ in enumerate((4096, 5632, 7168)):
        w384[i_], Bw384[i_] = av(o_, [128, 384])
    sq0, Bsq0 = av(0, [128, 384])
    sq97, Bsq97 = av(0, [128, 4, 97])
    PTm4, BPTm4 = av(0, [128, 4, 128], BF16)
    vp, Bvp = av(1024, [128, 4, 97], BF16)
    ktok, Bktok = av(1856, [128, 384], BF16)
    nd, Bnd = av(7168, [128, 4, 97])
    vn, Bvn = av(8768, [128, 256], BF16)
    ya, Bya = av(9280, [128, 256], BF16)
    yb, Byb = av(9792, [128, 384], BF16)
    qn16, Bqn16 = av(10560, [128, 384], BF16)
    kn16, Bkn16 = av(1536, [128, 384], BF16)
    ebuf = [None] * 4; Be = [None] * 4
    xbuf = [None] * 2; Bxb = [None] * 2; spb = [None] * 2; Bsp = [None] * 2; aTb = [None] * 2; BaT = [None] * 2; Lsum = [None] * 2; BLs = [None] * 2
    for i_ in range(4):
        ebuf[i_], Be[i_] = av(i_ * 1024, [128, GW])
    for i_ in range(2):
        xbuf[i_], Bxb[i_] = av(4096 + i_ * 1024, [128, GW])
        spb[i_], Bsp[i_] = av(6144 + i_ * 512, [128, GW], BF16)
        aTb[i_], BaT[i_] = av(7168 + i_ * 512, [128, GW], BF16)
        Lsum[i_], BLs[i_] = av(8192 + i_ * 512, [128, GW], BF16)
    yc, Byc_all = av(9216, [128, GT, 384], BF16)
    Byc = [[Byc_all for h in range(6)] for r in range(GT)]
    gate_bc, Bgate = av(4096, [128, D])
    gcol, Bgcol = av(8192, [128, 128])
    tmp5 = [None] * 2; Btmp5 = [None] * 2
    for i_ in range(2):
        tmp5[i_], Btmp5[i_] = av(i_ * 2048, [128, 512])
    ycT = hT
    print("sbuf bytes remaining", nc.sbuf_bytes_remaining)

    sc.op("pool", lambda: G.memset(vaug[:], 1.0), [], [Bvaug])
    sc.op("pool", lambda: G.memset(qTa[:], 0.0), [], BqT)
    sc.op("pool", lambda: G.memset(qTb[:], 0.0), [], BqT)

    def proj_tok(r, c0, c1, bank):
        n = c1 - c0
        for kc in range(8):
            sc.op("pe", lambda: T.matmul(PB[bank][:, 0:n], lhsT=hT[:, kc, r * 128:(r + 1) * 128], rhs=win[:, kc, c0:c1], start=(kc == 0), stop=(kc == 7)),
                  [BhT[r], Bwin], pq(bank, 0, n))

    def rstd_small(src_ap, dst_ap, n_inv, reads, writes):
        sc.op("act", lambda: A.activation(out=dst_ap, in_=src_ap, func=AF.Ln, scale=n_inv, bias=EPS), reads, writes)
        sc.op("act", lambda: A.activation(out=dst_ap, in_=dst_ap, func=AF.Exp, scale=-0.5), writes, writes)

    PBbf = {4: PB[4][:].bitcast(BF16), 2: PB[2][:].bitcast(BF16)}
    TRB = [4, 2]
    trn = [0]

    def tr_batch(srcs, reads):
        bank = TRB[trn[0] % 2]
        trn[0] += 1
        off = 0
        for (src, pin, cin) in srcs:
            o_ = off
            sc.op("pe", lambda: T.transpose(PBbf[bank][0:cin, o_:o_ + pin], src, identB[0:pin, 0:pin]), list(reads) + [BcB], pq(bank, 0, 1))
            off += pin
        return bank, PBbf[bank]

    tslot = [0]

    def next_tslot():
        tslot[0] = (tslot[0] + 1) % 8
        return tslot[0]

    def stop_at(tag):
        if STOP == tag:
            raise _Stop()

    win_loaded = {(0, 0): True}
    order = [(b_, l_) for b_ in range(NSEQ) for l_ in range(DEPTH)]

    try:
        for b in range(NSEQ):
          for l in range(DEPTH):
              last = (l == DEPTH - 1)
              if not win_loaded.get((b, l)):
                  load_win(l)
              sc.dma("pool", "wout", wout[:], wout_d[l].rearrange("(k p) n -> p k n", p=128), writes=[Bwout])
              sc.dma("sp", "p0", wsTf[:], wsT_d[l], writes=[BwsTf])
              sc.dma("sp", "p1", gainA[:], gav_d[l:l + 1, :].partition_broadcast(128), writes=[BgainA])
              sc.dma("sp", "p2", bsTs[:], bsT_d[:, l, :], writes=[BbsT])
              sc.dma("sp", "p3", convw[:], convw_d[:, l, :, :], writes=[Bconvw])
              sc.dma("sp", "p4", convb[:], convb_d[:, l, :], writes=[Bconvb])
              sc.dma("sp", "p5", gbias[:], gbias_d[l:l + 1, :].partition_broadcast(128), writes=[Bgbias])
              sc.dma("sp", "p6", hnorm[:], hnorm_d[l:l + 1, :].partition_broadcast(128), writes=[Bhnorm])
              sc.dma("sp", "p7", qnb[:], qnorm_d[l:l + 1, :].partition_broadcast(128), writes=[Bqnb])
              sc.dma("sp", "p8", knb[:], knorm_d[l:l + 1, :].partition_broadcast(128), writes=[Bknb])
              for g in range(4):
                  sc.op("dve", lambda: V.tensor_tensor(out=wsTb[:, g, :], in0=wsTf[:, g, :], in1=triU, op=ALU.mult), [BwsTf, BcF], [BwsTb])
              sc.op("dve", lambda: V.tensor_scalar_mul(out=gainA[:], in0=gainA[:], scalar1=0.5), [BgainA], [BgainA])
              sc.op("dve", lambda: V.tensor_scalar_mul(out=bsTs[:], in0=bsTs[:], scalar1=0.5), [BbsT], [BbsT])
              sc.op("dve", lambda: V.tensor_scalar_mul(out=hnorm[:], in0=hnorm[:], scalar1=0.25), [Bhnorm], [Bhnorm])
              sc.op("dve", lambda: V.tensor_scalar_mul(out=qnb[:], in0=qnb[:], scalar1=0.125), [Bqnb], [Bqnb])
              for c in range(8):
                  bank = c // 4
                  cs = (c % 4) * 128
                  sc.op("dve", lambda: V.tensor_scalar_mul(out=gcol[:], in0=onesF, scalar1=modT[:, l, 16 + c, b:b + 1]), [BcF, BmodT], [Bgcol])
                  sc.op("pe", lambda: T.matmul(PB[bank][:, cs:cs + 128], lhsT=gcol[:], rhs=identF, start=True, stop=True), [Bgcol, BcF], pq(bank, cs, cs + 128))
                  sc.op("act", lambda: A.copy(out=gate_bc[:, c * 128:(c + 1) * 128], in_=PB[bank][:, cs:cs + 128]), pq(bank, cs, cs + 128), [Bgate])
              for kc in range(8):
                  sc.op("pool", lambda: G.tensor_tensor(out=wout[:, kc, :], in0=wout[:, kc, :], in1=gate_bc[:], op=ALU.mult), [Bwout, Bgate], [Bwout])
              sc.op("pool", lambda: G.memset(C32[:], 0.0), [], BC32)
              sc.op("pool", lambda: G.memset(Cb[:], 0.0), [], BCb)
              sc.op("pool", lambda: G.memset(hist[:], 0.0), [], Bhist)

              stop_at('g0')
              for I in range(NG):
                  tiles = [I * GT + r for r in range(GT)]
                  for r, t in enumerate(tiles):
                      sc.op("dve", lambda: V.memset(ss[:, 0:1], 0.0), [], [Bss])
                      sc.op("act", lambda: A.activation(out=xn[:], in_=xres[:, t, :], func=AF.Square, accum_out=ss[:, 0:1]), [Bx[t], Bss], [Bxn, Bss])
                      stop_at('g1a')
                      rstd_small(ss[:, 0:1], ss[:, 1:2], 1.0 / D, [Bss], [Bss])
                      stop_at('g1b')
                      sc.op("dve", lambda: V.tensor_scalar_mul(out=xn[:], in0=xres[:, t, :], scalar1=ss[:, 1:2]), [Bx[t], Bss], [Bxn])
                      stop_at('g1c')
                      for c in range(8):
                          bank = 2 + c % 2
                          cs = (c // 2) * 128
                          sc.op("pe", lambda: T.transpose(PB[bank][:, cs:cs + 128], xn[:, c * 128:(c + 1) * 128], identF), [Bxn, BcF], pq(bank, cs, cs + 128))
                          stop_at('g1d')
                          if c % 2 == 0:
                              sc.op("dve", lambda: V.tensor_scalar(out=hT[:, c, r * 128:(r + 1) * 128], in0=PB[bank][:, cs:cs + 128],
                                                                   scalar1=Amod[:, l, b, c:c + 1], scalar2=modT[:, l, c, b:b + 1], op0=ALU.mult, op1=ALU.add),
                                    pq(bank, cs, cs + 128) + [BAmod, BmodT], [BhT[r]])
                              stop_at('g1e')
                          else:
                              sc.op("act", lambda: A.activation(out=hT[:, c, r * 128:(r + 1) * 128], in_=PB[bank][:, cs:cs + 128], func=AF.Identity,
                                                                scale=Amod[:, l, b, c:c + 1], bias=modT[:, l, c, b:b + 1]),
                                    pq(bank, cs, cs + 128) + [BAmod, BmodT], [BhT[r]])
                              if (r, c) == tuple(int(v) for v in os.environ.get("KRC", "0,1").split(",")):
                                  stop_at('g1f')
                  stop_at('g1')
                  def unit_front(u):
                      col0 = 768 + u * 96
                      rb = u % 2
                      ub = 2 + u % 2
                      acc, Bacc = accs[rb], Baccs[rb]
                      for kc in range(8):
                          sc.op("pe", lambda: T.matmul(PB[ub][0:96, 0:GW], lhsT=win[:, kc, col0:col0 + 96], rhs=hT[:, kc, :], start=(kc == 0), stop=(kc == 7)),
                                BhT + [Bwin], pq(ub, 0, GW))
                      sc.op("pool", lambda: G.tensor_copy(out=raw[rb][:, 0:3], in_=hist[:, u, :]), [Bhist[u]], [Braw[rb]])
                      sc.op("act", lambda: A.copy(out=raw[rb][:, 3:3 + GW], in_=PB[ub][0:96, 0:GW]), pq(ub, 0, GW), [Braw[rb]])
                      sc.op("pool", lambda: G.tensor_copy(out=hist[:, u, :], in_=raw[rb][:, GW:GW + 3]), [Braw[rb]], [Bhist[u]])
                      sc.op("act", lambda: A.activation(out=acc[:], in_=raw[rb][:, 3:3 + GW], func=AF.Identity, scale=convw[:, u, 3:4], bias=convb[:, u:u + 1]),
                            [Braw[rb], Bconvw, Bconvb], [Bacc])

                  def unit_back(u):
                      rb = u % 2
                      acc, Bacc, thq, Bthq = accs[rb], Baccs[rb], thqs[rb], Bthqs[rb]
                      if u in (2, 6):
                          for tap in (2, 1, 0):
                              sc.op("pool", lambda: G.tensor_scalar_mul(out=tmpP[:], in0=raw[rb][:, tap:tap + GW], scalar1=convw[:, u, tap:tap + 1]), [Braw[rb], Bconvw], [BtmpP])
                              sc.op("pool", lambda: G.tensor_tensor(out=acc[:], in0=acc[:], in1=tmpP[:], op=ALU.add), [Bacc, BtmpP], [Bacc])
                      else:
                          for tap in (2, 1, 0):
                              sc.op("dve", lambda: V.scalar_tensor_tensor(out=acc[:], in0=raw[rb][:, tap:tap + GW], scalar=convw[:, u, tap:tap + 1], in1=acc[:], op0=ALU.mult, op1=ALU.add),
                                    [Braw[rb], Bconvw, Bacc], [Bacc])
                      sc.op("act", lambda: A.activation(out=thq[:], in_=acc[:], func=AF.Tanh, scale=0.5), [Bacc], [Bthq])
                      sc.op("dve", lambda: V.scalar_tensor_tensor(out=qkT[:, u, :], in0=thq[:], scalar=1.0, in1=acc[:], op0=ALU.add, op1=ALU.mult), [Bthq, Bacc], [BqkT[u]])

                  unit_front(0)
                  for u in range(8):
                      if u + 1 < 8:
                          unit_front(u + 1)
                      unit_back(u)
                  stop_at('g2')
                  for r, t in enumerate(tiles):
                      rc = slice(r * 128, (r + 1) * 128)
                      guv, zgA, tmpA = w512[0], w512[1], w512[2]
                      proj_tok(r, 0, 512, 0)
                      sc.op("act", lambda: A.activation(out=guv[:], in_=PB[0][:, 0:512], func=AF.Gelu_apprx_tanh), pq(0, 0, 512), [Bw512[0]])
                      proj_tok(r, 512, 768, 1)
                      sc.op("act", lambda: A.activation(out=tmpA[:, 0:256], in_=PB[1][:, 0:256], func=AF.Tanh, scale=0.5), pq(1, 0, 256), [Bw512[2]])
                      sc.op("dve", lambda: V.scalar_tensor_tensor(out=zgA[:, 0:256], in0=tmpA[:, 0:256], scalar=1.0, in1=PB[1][:, 0:256], op0=ALU.add, op1=ALU.mult),
                            [Bw512[2]] + pq(1, 0, 256), [Bw512[1]])
                      sc.op("dve", lambda: V.tensor_tensor(out=zgA[:, 0:256], in0=zgA[:, 0:256], in1=guv[:, 0:256], op=ALU.mult), [Bw512[1], Bw512[0]], [Bw512[1]])
                      to_, ozg, tzb = w384[0], w384[1], w384[2]
                      proj_tok(r, 1920, 2304, 0)
                      sc.op("act", lambda: A.activation(out=to_[:], in_=PB[0][:, 0:384], func=AF.Tanh, scale=0.5), pq(0, 0, 384), [Bw384[0]])
                      proj_tok(r, 2304, 2696, 1)
                      sc.op("act", lambda: A.activation(out=tzb[:], in_=PB[1][:, 0:384], func=AF.Tanh, scale=0.5), pq(1, 0, 384), [Bw384[2]])
                      sc.op("dve", lambda: V.scalar_tensor_tensor(out=ozg[:], in0=tzb[:], scalar=1.0, in1=PB[1][:, 0:384], op0=ALU.add, op1=ALU.mult),
                            [Bw384[2]] + pq(1, 0, 384), [Bw384[1]])
                      sc.op("dve", lambda: V.scalar_tensor_tensor(out=ozg[:], in0=to_[:], scalar=1.0, in1=ozg[:], op0=ALU.add, op1=ALU.mult), [Bw384[0], Bw384[1]], [Bw384[1]])
                      sc.op("pool", lambda: G.tensor_tensor(out=ozg[:], in0=ozg[:], in1=hnorm[:], op=ALU.mult), [Bw384[1], Bhnorm], [Bw384[1]])
                      sc.op("dve", lambda: V.tensor_tensor(out=gt[:], in0=PB[1][:, 384:392], in1=gbias[:], op=ALU.add), pq(1, 384, 392) + [Bgbias], [Bgt])
                      proj_tok(r, 3848, 4232, 0)
                      sc.op("act", lambda: A.activation(out=to_[:], in_=PB[0][:, 0:384], func=AF.Tanh, scale=0.5), pq(0, 0, 384), [Bw384[0]])
                      sc.op("dve", lambda: V.scalar_tensor_tensor(out=szc[:, r, :], in0=to_[:], scalar=1.0, in1=PB[0][:, 0:384], op0=ALU.add, op1=ALU.mult),
                            [Bw384[0]] + pq(0, 0, 384), [Bszc[r]])
                      proj_tok(r, 1536, 1920, 1)
                      sc.op("act", lambda: A.copy(out=vaug[:, :, 0:96], in_=PB[1][:, 0:384].rearrange("p (h d) -> p h d", h=4)), pq(1, 0, 384), [Bvaug])
                      proj_tok(r, 3464, 3848, 0)
                      sc.op("act", lambda: A.copy(out=vc[:, t, :], in_=PB[0][:, 0:384]), pq(0, 0, 384), [Bvc[t]])
                      proj_tok(r, 2696, 3080, 1)
                      proj_tok(r, 3080, 3464, 0)
                      sc.op("dve", lambda: V.tensor_tensor(out=tmpA[:, 0:256], in0=guv[:, 256:512], in1=guv[:, 256:512], op=ALU.mult), [Bw512[0]], [Bw512[2]])
                      sc.op("dve", lambda: V.reduce_sum(out=ss[:, 2:6], in_=tmpA[:, 0:256].rearrange("p (g d) -> p g d", g=4), axis=AX.X), [Bw512[2]], [Bss])
                      rstd_small(ss[:, 2:6], ss[:, 2:6], 1.0 / 64, [Bss], [Bss])
                      sc.op("dve", lambda: V.tensor_tensor(out=tmpA[:, 0:256].rearrange("p (g d) -> p g d", g=4), in0=guv[:, 256:512].rearrange("p (g d) -> p g d", g=4),
                                                           in1=ss[:, 2:6].unsqueeze(2).to_broadcast([128, 4, 64]), op=ALU.mult), [Bw512[0], Bss], [Bw512[2]])
                      sc.op("dve", lambda: V.tensor_tensor(out=vn[:], in0=tmpA[:, 0:256], in1=gainA[:], op=ALU.mult), [Bw512[2], BgainA], [Bvn])
                      for g in range(4):
                          sc.op("pe", lambda: T.matmul(PB[7][:, g * 64:(g + 1) * 64], lhsT=wsTb[:, g, :], rhs=vn[:, g * 64:(g + 1) * 64], start=True, stop=True),
                                [BwsTb, Bvn], pq(7, g * 64, g * 64 + 64))
                      for g in range(4):
                          sc.op("dve", lambda: V.scalar_tensor_tensor(out=ya[:, g * 64:(g + 1) * 64], in0=PB[7][:, g * 64:(g + 1) * 64], scalar=bsTs[:, g:g + 1],
                                                                      in1=zgA[:, g * 64:(g + 1) * 64], op0=ALU.add, op1=ALU.mult),
                                pq(7, g * 64, g * 64 + 64) + [BbsT, Bw512[1]], [Bya])
                      sc.op("act", lambda: A.activation(out=sm[:, 0:4], in_=gt[:, 4:8], func=AF.Exp, scale=-1.0), [Bgt], [Bsm])
                      sc.op("act", lambda: A.activation(out=sm[:, 0:4], in_=sm[:, 0:4], func=AF.Ln, bias=1.0), [Bsm], [Bsm])
                      sc.op("pe", lambda: T.matmul(PB[6][:, 400:404], lhsT=triU, rhs=sm[:, 0:4], start=True, stop=True), [BcF, Bsm], pq(6, 400, 404))
                      sc.op("pe", lambda: T.matmul(PB[6][:, 404:408], lhsT=onesF, rhs=sm[:, 0:4], start=True, stop=True), [BcF, Bsm], pq(6, 404, 408))
                      sc.op("dve", lambda: V.tensor_copy(out=sm[:, 8:16], in_=PB[6][:, 400:408]), pq(6, 400, 408), [Bsm])
                      sc.op("dve", lambda: V.tensor_tensor(out=sm[:, 16:20], in0=gt[:, 0:4], in1=sm[:, 8:12], op=ALU.add), [Bgt, Bsm], [Bsm])
                      sc.op("act", lambda: A.activation(out=sm[:, 16:20], in_=sm[:, 16:20], func=AF.Exp), [Bsm], [Bsm])
                      sc.op("act", lambda: A.activation(out=sm[:, 24:32], in_=sm[:, 8:16], func=AF.Exp, scale=-1.0), [Bsm], [Bsm])
                      sc.op("dve", lambda: V.tensor_tensor(out=vp[:], in0=vaug[:], in1=sm[:, 16:20].unsqueeze(2).to_broadcast([128, 4, 97]), op=ALU.mult), [Bvaug, Bsm], [Bvp])
                      for h in range(4):
                          sc.op("pe", lambda: T.matmul(PB[3][:, h * 128:(h + 1) * 128], lhsT=qkT[:, 4 + h, rc], rhs=qkT[:, h, rc], start=True, stop=True),
                                [BqkT[4 + h], BqkT[h]], pq(3, 0, 512))
                      sc.op("dve", lambda: V.tensor_tensor(out=PTm4[:], in0=PB[3][:, 0:512].rearrange("p (h t) -> p h t", h=4),
                                                           in1=triU.unsqueeze(1).to_broadcast([128, 4, 128]), op=ALU.mult), pq(3, 0, 512) + [BcF], [BPTm4])
                      for h in range(4):
                          hv = slice(h * 97, (h + 1) * 97)
                          sc.op("pe", lambda: T.matmul(PB[5][:, hv], lhsT=PTm4[:, h, :], rhs=vp[:, h, :], start=True, stop=False), [BPTm4, Bvp], pq(5, 0, 388))
                          sc.op("pe", lambda: T.matmul(PB[5][:, hv], lhsT=qkT[:, h, rc], rhs=Cb[:, h, :], start=False, stop=True), [BqkT[h]] + BCb, pq(5, 0, 388))
                      kb, kps = tr_batch([(qkT[:, 4 + h, rc], 96, 128) for h in range(4)], [BqkT[4 + h] for h in range(4)])
                      sc.op("act", lambda: A.copy(out=ktok[:], in_=kps[:, 0:384]), pq(kb, 0, 1), [Bktok])
                      for h in range(4):
                          hv = slice(h * 97, (h + 1) * 97)
                          sc.op("pe", lambda: T.matmul(PB[7][0:96, hv], lhsT=ktok[:, h * 96:(h + 1) * 96], rhs=vp[:, h, :], start=True, stop=True), [Bktok, Bvp], pq(7, 0, 388))
                      sc.op("dve", lambda: V.tensor_tensor(out=nd[:], in0=PB[5][:, 0:388].rearrange("p (h d) -> p h d", h=4),
                                                           in1=sm[:, 24:28].unsqueeze(2).to_broadcast([128, 4, 97]), op=ALU.mult), pq(5, 0, 388) + [Bsm], [Bnd])
                      sc.op("dve", lambda: V.scalar_tensor_tensor(out=sm[:, 32:36], in0=nd[:, :, 96], scalar=-1.0, in1=nd[:, :, 96], op0=ALU.mult, op1=ALU.max), [Bnd], [Bsm])
                      sc.op("dve", lambda: V.tensor_scalar_max(out=sm[:, 32:36], in0=sm[:, 32:36], scalar1=CS_INV), [Bsm], [Bsm])
                      sc.op("dve", lambda: V.reciprocal(out=sm[:, 32:36], in_=sm[:, 32:36]), [Bsm], [Bsm])
                      sc.op("dve", lambda: V.tensor_tensor(out=sq97[:], in0=nd[:], in1=nd[:], op=ALU.mult), [Bnd], [Bsq97])
                      sc.op("dve", lambda: V.reduce_sum(out=sm[:, 36:40], in_=sq97[:, :, 0:96], axis=AX.X), [Bsq97], [Bsm])
                      sc.op("dve", lambda: V.tensor_tensor(out=sm[:, 40:44], in0=sm[:, 32:36], in1=sm[:, 32:36], op=ALU.mult), [Bsm], [Bsm])
                      sc.op("dve", lambda: V.tensor_tensor(out=sm[:, 40:44], in0=sm[:, 40:44], in1=sm[:, 36:40], op=ALU.mult), [Bsm], [Bsm])
                      rstd_small(sm[:, 40:44], sm[:, 40:44], 1.0 / 96, [Bsm], [Bsm])
                      sc.op("dve", lambda: V.tensor_tensor(out=sm[:, 44:48], in0=sm[:, 40:44], in1=sm[:, 32:36], op=ALU.mult), [Bsm], [Bsm])
                      for h in range(4):
                          sc.op("dve", lambda: V.scalar_tensor_tensor(out=yb[:, h * 96:(h + 1) * 96], in0=nd[:, h, 0:96], scalar=sm[:, 44 + h:45 + h], in1=ozg[:, h * 96:(h + 1) * 96],
                                                                      op0=ALU.mult, op1=ALU.mult), [Bnd, Bsm, Bw384[1]], [Byb])
                      sc.op("dve", lambda: V.tensor_tensor(out=C32[:], in0=C32[:], in1=PB[7][0:96, 0:388].rearrange("p (h d) -> p h d", h=4), op=ALU.add),
                            BC32 + pq(7, 0, 388), BC32)
                      sc.op("pool", lambda: G.tensor_tensor(out=C32[:], in0=C32[:], in1=sm[0:96, 28:32].unsqueeze(2).to_broadcast([96, 4, 97]), op=ALU.mult), BC32 + [Bsm], BC32)
                      sc.op("pool", lambda: G.tensor_copy(out=Cb[:], in_=C32[:]), BC32, BCb)
                      cbuf = [(1, sq0, Bsq0, w384[0], Bw384[0], qn16, Bqn16, qnb, Bqnb, 2), (0, w384[1], Bw384[1], w384[2], Bw384[2], kn16, Bkn16, knb, Bknb, 8)]
                      for (pb_, sqx, Bsqx, fx, Bfx, n16, Bn16, gsrc, Bg, so) in cbuf:
                          sc.op("act", lambda: A.activation(out=sqx[:], in_=PB[pb_][:, 0:384], func=AF.Square), pq(pb_, 0, 384), [Bsqx])
                      for (pb_, sqx, Bsqx, fx, Bfx, n16, Bn16, gsrc, Bg, so) in cbuf:
                          sc.op("dve", lambda: V.reduce_sum(out=ss[:, so:so + 6], in_=sqx[:].rearrange("p (h d) -> p h d", h=6), axis=AX.X), [Bsqx], [Bss])
                      rstd_small(ss[:, 2:14], ss[:, 2:14], 1.0 / 64, [Bss], [Bss])
                      for (pb_, sqx, Bsqx, fx, Bfx, n16, Bn16, gsrc, Bg, so) in cbuf:
                          sc.op("dve", lambda: V.tensor_tensor(out=fx[:].rearrange("p (h d) -> p h d", h=6), in0=PB[pb_][:, 0:384].rearrange("p (h d) -> p h d", h=6),
                                                               in1=ss[:, so:so + 6].unsqueeze(2).to_broadcast([128, 6, 64]), op=ALU.mult), pq(pb_, 0, 384) + [Bss], [Bfx])
                      for (pb_, sqx, Bsqx, fx, Bfx, n16, Bn16, gsrc, Bg, so) in cbuf:
                          sc.op("dve" if pb_ else "pool", (lambda: V.tensor_tensor(out=n16[:].rearrange("p (h d) -> p h d", h=6), in0=fx[:].rearrange("p (h d) -> p h d", h=6),
                                                                                    in1=gsrc[:].unsqueeze(1).to_broadcast([128, 6, 64]), op=ALU.mult)) if pb_ else
                                (lambda: G.tensor_tensor(out=n16[:].rearrange("p (h d) -> p h d", h=6), in0=fx[:].rearrange("p (h d) -> p h d", h=6),
                                                         in1=gsrc[:].unsqueeze(1).to_broadcast([128, 6, 64]), op=ALU.mult)), [Bfx, Bg], [Bn16])
                      tb, tps_ = tr_batch([(qn16[:, p * 128:(p + 1) * 128], 128, 128) for p in range(3)], [Bqn16])
                      sc.op("dve", lambda: V.tensor_copy(out=qTa[0:64, :, rc], in_=tps_[0:64, 0:384].rearrange("p (c t) -> p c t", c=3)), pq(tb, 0, 1), [BqT[r]])
                      sc.op("act", lambda: A.copy(out=qTb[64:128, :, rc], in_=tps_[64:128, 0:384].rearrange("p (c t) -> p c t", c=3)), pq(tb, 0, 1), [BqT[r]])
                      tb, tps_ = tr_batch([(kn16[:, p * 128:(p + 1) * 128], 128, 128) for p in range(3)], [Bkn16])
                      sc.op("act", lambda: A.copy(out=kT[:, :, t * 128:(t + 1) * 128], in_=tps_[:, 0:384].rearrange("p (c t) -> p c t", c=3)), pq(tb, 0, 1), [BkT[t]])
                      tb, tps_ = tr_batch([(ya[:, j * 128:(j + 1) * 128], 128, 128) for j in range(2)] + [(yb[:, j * 128:(j + 1) * 128], 128, 128) for j in range(3)], [Bya, Byb])
                      sc.op("dve", lambda: V.tensor_copy(out=ycT[:, 0:5, rc], in_=tps_[:, 0:640].rearrange("p (c t) -> p c t", c=5)), pq(tb, 0, 1), [BhT[r]])
                  stop_at('g3')
                  if I == NG - 1:
                      nxt = order.index((b, l)) + 1
                      if nxt < len(order):
                          load_win(order[nxt][1])
                          win_loaded[order[nxt]] = True
                  jmax = I * GT + GT - 1
                  its = []
                  for hp in range(3):
                      for j in range(jmax, -1, -1):
                          for e_ in range(2):
                              its.append((hp * 2 + e_, j))
                  ZB = [5, 3, 0]
                  CBk = [6, 2]
                  nit = len(its)

                  def geo(n):
                      hd, j = its[n]
                      r0 = max(0, j - I * GT)
                      return hd, j, hd // 2, hd % 2, r0, r0 * 128, GW - r0 * 128

                  for e_ in range(2):
                      sc.op("dve", lambda: V.memset(PB[7][:, e_ * 128:(e_ + 1) * 128], 0.0), [], pq(7, 0, 128))
                  for k in range(nit + 5):
                      if k < nit:
                          hd, j, p, e_, r0, t0, N = geo(k)
                          diag = j >= I * GT
                          qs = (qTa if e_ == 0 else qTb)[:, p, t0:GW]
                          zb = ZB[k % 3]
                          sc.op("pe", lambda: T.matmul(PB[zb][:, 0:N], lhsT=kT[:, p, j * 128:(j + 1) * 128], rhs=qs, start=True, stop=not diag), [BkT[j]] + BqT, pq(zb, 0, N))
                          if diag:
                              sc.op("pe", lambda: T.matmul(PB[zb][:, 0:128], lhsT=identB, rhs=nmaskS, start=False, stop=True), [BcB], pq(zb, 0, 128))
                      n = k - 1
                      if 0 <= n < nit:
                          hd, j, p, e_, r0, t0, N = geo(n)
                          zb = ZB[n % 3]
                          sc.op("act", lambda: A.activation(out=ebuf[n % 4][:, 0:N], in_=PB[zb][:, 0:N], func=AF.Exp), pq(zb, 0, N), [Be[n % 4]])
                      n = k - 2
                      if 0 <= n < nit:
                          hd, j, p, e_, r0, t0, N = geo(n)
                          cb = CBk[n % 2]
                          carry = j < jmax
                          sc.op("pe", lambda: T.matmul(PB[cb][:, 0:N], lhsT=nTriL, rhs=spb[n % 2][:, 0:N], start=True, stop=not carry), [BcB, Bsp[n % 2]], pq(cb, 0, N))
                          if carry:
                              sc.op("pe", lambda: T.matmul(PB[cb][:, 0:N], lhsT=nOnes, rhs=Lsum[e_][:, t0:GW], start=False, stop=True), [BcB, BLs[e_]], pq(cb, 0, N))
                          if j == jmax:
                              sc.op("pool", lambda: G.memset(Lsum[e_][:], 0.0), [], [BLs[e_]])
                          if j > 0:
                              sc.op("pool", lambda: G.tensor_tensor(out=Lsum[e_][:, t0:GW], in0=Lsum[e_][:, t0:GW], in1=spb[n % 2][:, 0:N], op=ALU.add), [BLs[e_], Bsp[n % 2]], [BLs[e_]])
                      n = k - 3
                      if 0 <= n < nit:
                          hd, j, p, e_, r0, t0, N = geo(n)
                          cb = CBk[n % 2]
                          sc.op("act", lambda: A.activation(out=xbuf[n % 2][:, 0:N], in_=PB[cb][:, 0:N], func=AF.Exp), pq(cb, 0, N), [Bxb[n % 2]])
                          sc.op("dve", lambda: V.tensor_tensor(out=aTb[n % 2][:, 0:N], in0=xbuf[n % 2][:, 0:N], in1=ebuf[n % 4][:, 0:N], op=ALU.mult), [Bxb[n % 2], Be[n % 4]], [BaT[n % 2]])
                      n = k - 1
                      if 0 <= n < nit:
                          hd, j, p, e_, r0, t0, N = geo(n)
                          sc.op("act", lambda: A.activation(out=spb[n % 2][:, 0:N], in_=ebuf[n % 4][:, 0:N], func=AF.Ln, bias=1.0), [Be[n % 4]], [Bsp[n % 2]])
                      n = k - 4
                      if 0 <= n < nit:
                          hd, j, p, e_, r0, t0, N = geo(n)
                          for rr in range(r0, GT):
                              oc = e_ * 128 + rr * 64
                              sc.op("pe", lambda: T.matmul(PB[7][:, oc:oc + 64], lhsT=aTb[n % 2][:, (rr - r0) * 128:(rr - r0 + 1) * 128], rhs=vc[:, j, hd * 64:(hd + 1) * 64],
                                                           start=False, stop=(j == 0), skip_group_check=True), [BaT[n % 2], Bvc[j]], pq(7, oc, oc + 64))
                          if j == 0:
                              for rr in range(GT):
                                  oc = e_ * 128 + rr * 64
                                  sc.op("dve", lambda: V.scalar_tensor_tensor(out=yc[:, rr, hd * 64:(hd + 1) * 64], in0=PB[7][:, oc:oc + 64], scalar=0.5,
                                                                              in1=szc[:, rr, hd * 64:(hd + 1) * 64], op0=ALU.mult, op1=ALU.mult),
                                        pq(7, oc, oc + 64) + [Bszc[rr]], [Byc[rr][hd]])
                              if hd < 4:
                                  sc.op("dve", lambda: V.memset(PB[7][:, e_ * 128:(e_ + 1) * 128], 0.0), [], pq(7, 0, 128))
                  stop_at('g4')
                  for r, t in enumerate(tiles):
                      rc = slice(r * 128, (r + 1) * 128)
                      tb, tps_ = tr_batch([(yc[:, r, j * 128:(j + 1) * 128], 128, 128) for j in range(3)], [Byc_all])
                      sc.op("act", lambda: A.copy(out=ycT[:, 5:8, rc], in_=tps_[:, 0:384].rearrange("p (c t) -> p c t", c=3)), pq(tb, 0, 1), [BhT[r]])
                      if debug and b == 0 and l == 0 and I == 0 and r == GT - 1:
                          for c in range(8):
                              sc.op("dve", lambda: V.tensor_copy(out=tmp5[0][:, 0:GW], in_=ycT[:, c, :]), BhT, [Btmp5[0]])
                              sc.dma("sp", "dbg", dbg_d[:, c, :], tmp5[0][:, 0:GW], reads=[Btmp5[0]])
                      for half in range(2):
                          bank = half
                          hs = slice(half * 512, (half + 1) * 512)
                          for kc in range(8):
                              sc.op("pe", lambda: T.matmul(PB[bank][:, 0:512], lhsT=ycT[:, kc, rc], rhs=wout[:, kc, hs], start=(kc == 0), stop=(kc == 7)),
                                    [BhT[r], Bwout], pq(bank, 0, 512))
                          sc.op("dve", lambda: V.tensor_tensor(out=xres[:, t, hs], in0=PB[bank][:, 0:512], in1=xres[:, t, hs], op=ALU.add), pq(bank, 0, 512) + [Bx[t]], [Bx[t]])
                      if last:
                          sc.dma("sp", "o%d" % t, y_d[b, t * 128:(t + 1) * 128, :], xres[:, t, :], reads=[Bx[t]])
                          if b + 1 < NSEQ:
                              sc.dma("sp", "x%d" % t, xres[:, t, :], x_d[b + 1, t * 128:(t + 1) * 128, :], writes=[Bx[t]])
    except _Stop:
        pass
    sc.barrier()
    print("instructions", sc.ninst, "waits", sc.nwaits)
    return nc


def host_inputs(inp, core, NSEQ, DEPTH):
    f = lambda a: np.ascontiguousarray(np.asarray(a, dtype=np.float32))
    b0 = core * NSEQ
    m = {}
    m["x"] = f(inp["x"][b0:b0 + NSEQ])
    c = np.asarray(inp["c"], np.float32)[b0:b0 + NSEQ]
    m["cT"] = f(c.reshape(NSEQ, 8, 128).transpose(2, 1, 0))
    m["w_ada"] = f(inp["w_ada"][:DEPTH])
    m["b_adaT"] = f(np.asarray(inp["b_ada"], np.float32)[:DEPTH].reshape(DEPTH, 24, 128).transpose(2, 0, 1))
    m["normgT"] = f(np.asarray(inp["norm_g"], np.float32)[:DEPTH].reshape(DEPTH, 8, 128).transpose(2, 0, 1))
    m["w_in"] = f(inp["w_in"][:DEPTH])
    m["w_out"] = f(inp["w_out"][:DEPTH])
    m["gav"] = f(np.asarray(inp["ga_v_norm"], np.float32)[:DEPTH].reshape(DEPTH, 256))
    m["wsT"] = f(np.asarray(inp["ga_ws"], np.float32)[:DEPTH].transpose(0, 3, 1, 2))
    m["bsT"] = f(np.asarray(inp["ga_bs"], np.float32)[:DEPTH].transpose(2, 0, 1))
    cw = np.asarray(inp["mb_conv_w"], np.float32)[:DEPTH]
    m["convw"] = f(cw.reshape(DEPTH, 4, 8, 96).transpose(3, 0, 2, 1))
    m["convb"] = f(np.asarray(inp["mb_conv_b"], np.float32)[:DEPTH].reshape(DEPTH, 8, 96).transpose(2, 0, 1))
    m["gbias"] = f(np.concatenate([np.asarray(inp["mb_b_i"], np.float32)[:DEPTH], np.asarray(inp["mb_b_f"], np.float32)[:DEPTH]], axis=1))
    m["hnorm"] = f(np.asarray(inp["mb_h_norm"], np.float32)[:DEPTH].reshape(DEPTH, 384))
    m["qnorm"] = f(np.asarray(inp["sc_q_norm"], np.float32)[:DEPTH])
    m["knorm"] = f(np.asarray(inp["sc_k_norm"], np.float32)[:DEPTH])
    i = np.arange(128)
    cs = np.zeros((128, 8, 128), np.float32)
    cs[:, 0] = np.eye(128)
    cs[:, 1] = (i[:, None] <= i[None, :])
    cs[:, 2] = np.where(i[:, None] <= i[None, :], 0.0, BIG)
    cs[:, 3] = 1.0
    cs[:, 4] = np.eye(128)
    cs[:, 5] = -(i[:, None] >= i[None, :]).astype(np.float32)
    cs[:, 6] = np.where(i[:, None] < i[None, :], 0.0, -BIG)
    cs[:, 7] = -1.0
    m["consts"] = cs
    return m


_NC_CACHE = {}


def run(inp, NSEQ, S, DEPTH, ncores, debug=False):
    key = (NSEQ, S, DEPTH, debug)
    if key not in _NC_CACHE:
        _NC_CACHE[key] = build(NSEQ, S, DEPTH, debug)
    nc = _NC_CACHE[key]
    in_maps = [host_inputs(inp, c, NSEQ, DEPTH) for c in range(ncores)]
    res = run_bass_kernel_spmd(nc, in_maps, core_ids=list(range(ncores)))
    out = np.concatenate([r["y"] for r in res.results], axis=0)
    if debug:
        return out, res.results[0]["dbg"]
    return out


def kernel(**inputs):
    out = run(inputs, 4, 2048, 2, 8)
    return np.ascontiguousarray(out.astype(np.float32))
```

```python
import numpy as np
import concourse.bass as bass
import concourse.mybir as mybir
from concourse.bass_utils import run_bass_kernel_spmd

F32 = mybir.dt.float32
BF16 = mybir.dt.bfloat16
AF = mybir.ActivationFunctionType
ALU = mybir.AluOpType
AX = mybir.AxisListType

D = 1024
DIN = 4232
EPS = 1e-6
CS_INV = 4.0 * (96.0 ** 0.5)
BIG = 30000.0


class Buf:
    __slots__ = ("name", "w", "r", "excl")

    def __init__(self, name, excl=False):
        self.name = name
        self.w = {}
        self.r = {}
        self.excl = excl


class Sched:
    def __init__(self, nc):
        self.nc = nc
        self.eng = {"pe": nc.tensor, "act": nc.scalar, "dve": nc.vector, "pool": nc.gpsimd, "sp": nc.sync}
        self.sems = {}
        self.cnt = {}
        self.known = {e: {} for e in self.eng}
        self._ctx = []
        for e in ["pe", "act", "dve", "pool"]:
            self._mksem(e)
        self.ninst = 0
        self.nwaits = 0

    def _mksem(self, key):
        cm = self.nc.semaphore("s_" + key)
        s = cm.__enter__()
        self._ctx.append(cm)
        self.sems[key] = s
        self.cnt[key] = 0

    def _wait(self, e, k, v):
        if self.known[e].get(k, 0) >= v:
            return
        self.eng[e].wait_ge(self.sems[k], v)
        self.known[e][k] = v
        self.nwaits += 1

    def _need(self, e, reads, writes):
        need = {}
        for b in reads:
            for k, v in b.w.items():
                if need.get(k, 0) < v:
                    need[k] = v
        for b in writes:
            for k, v in b.w.items():
                if k != e and need.get(k, 0) < v:
                    need[k] = v
            for k, v in b.r.items():
                if k != e and need.get(k, 0) < v:
                    need[k] = v
        return [(k, v) for k, v in need.items() if self.known[e].get(k, 0) < v]

    def _deps(self, e, reads, writes):
        for k, v in self._need(e, reads, writes):
            self._wait(e, k, v)

    def _record(self, k, v, reads, writes):
        for b in reads:
            if b.r.get(k, 0) < v:
                b.r[k] = v
        for b in writes:
            b.w[k] = v
            b.r = {}

    @staticmethod
    def _flat(lst):
        out = []
        for b in lst:
            if isinstance(b, (list, tuple)):
                out.extend(b)
            else:
                out.append(b)
        return out

    def op(self, e, fn, reads=(), writes=(), noinline=False):
        reads, writes = self._flat(reads), self._flat(writes)
        ex = [b for b in reads if b.excl]
        if ex:
            reads = [b for b in reads if not b.excl]
            writes = list(writes) + ex
        need = self._need(e, reads, writes)
        inline = None
        if need and e in ("dve", "pool", "act") and not noinline:
            inline = need.pop()
        for k, v in need:
            self._wait(e, k, v)
        ins = fn()
        if inline is not None:
            ins._wait_ge(self.sems[inline[0]], inline[1])
            self.known[e][inline[0]] = inline[1]
        self.cnt[e] += 1
        ins.then_inc(self.sems[e], 1)
        self._record(e, self.cnt[e], reads, writes)
        self.ninst += 1

    def dma(self, q, key, out, in_, reads=(), writes=()):
        if key not in self.sems:
            self._mksem(key)
        reads, writes = self._flat(reads), self._flat(writes)
        self._deps(q, reads, writes)
        ins = self.eng[q].dma_start(out=out, in_=in_)
        self.cnt[key] += 16
        ins.then_inc(self.sems[key], 16)
        self._record(key, self.cnt[key], reads, writes)
        self.ninst += 1

    def barrier(self):
        for e in self.eng:
            for k, v in self.cnt.items():
                if v > 0:
                    self._wait(e, k, v)


class _Stop(Exception):
    pass


def build(NSEQ, S, DEPTH, debug=False):
    import os
    STOP = os.environ.get("KSTOP", "")
    NT = S // 128
    GT = 2
    GW = GT * 128
    NG = NT // GT
    nc = bass.Bass("TRN2", target_bir_lowering=False)
    sc = Sched(nc)
    V, A, G, T = nc.vector, nc.scalar, nc.gpsimd, nc.tensor

    def dram(name, shape, dt=F32, kind="ExternalInput"):
        return nc.dram_tensor(name, list(shape), dt, kind=kind).ap()

    x_d = dram("x", [NSEQ, S, D])
    cT_d = dram("cT", [128, 8, NSEQ])
    wada_d = dram("w_ada", [DEPTH, D, 3 * D])
    badaT_d = dram("b_adaT", [128, DEPTH, 24])
    normgT_d = dram("normgT", [128, DEPTH, 8])
    win_d = dram("w_in", [DEPTH, D, DIN])
    wout_d = dram("w_out", [DEPTH, D, D])
    gav_d = dram("gav", [DEPTH, 256])
    wsT_d = dram("wsT", [DEPTH, 128, 4, 128])
    bsT_d = dram("bsT", [128, DEPTH, 4])
    convw_d = dram("convw", [96, DEPTH, 8, 4])
    convb_d = dram("convb", [96, DEPTH, 8])
    gbias_d = dram("gbias", [DEPTH, 8])
    hnorm_d = dram("hnorm", [DEPTH, 384])
    qnorm_d = dram("qnorm", [DEPTH, 64])
    knorm_d = dram("knorm", [DEPTH, 64])
    consts_d = dram("consts", [128, 8, 128])
    y_d = dram("y", [NSEQ, S, D], kind="ExternalOutput")
    Bwinbf = [[Buf("winbf%d_%d" % (l, i)) for i in range(16)] for l in range(DEPTH)]
    Bwoutbf = [[Buf("woutbf%d_%d" % (l, i)) for i in range(8)] for l in range(DEPTH)]
    if debug:
        dbg_d = dram("dbg", [128, 8, GW], kind="ExternalOutput")

    def sb(name, shape, dt=F32):
        return nc.alloc_sbuf_tensor("sb_" + name, list(shape), dt)

    cF = sb("cF", [128, 4, 128]); BcF = Buf("cF")
    cB = sb("cB", [128, 4, 128], BF16); BcB = Buf("cB")
    identF, triU, pmaskI, onesF = cF[:, 0, :], cF[:, 1, :], cF[:, 2, :], cF[:, 3, :]
    identB, nTriL, nmaskS, nOnes = cB[:, 0, :], cB[:, 1, :], cB[:, 2, :], cB[:, 3, :]
    modT = sb("modT", [128, DEPTH, 24, NSEQ]); BmodT = Buf("modT")
    Amod = sb("Amod", [128, DEPTH, NSEQ, 8]); BAmod = Buf("Amod")
    normgT = sb("normgT", [128, DEPTH, 8]); Bnormg = Buf("normg")
    badaT = sb("badaT", [128, DEPTH, 24]); Bbada = Buf("bada")
    cact = sb("cact", [128, 8, NSEQ]); Bcact = Buf("cact")
    ctmp = sb("ctmp", [128, 8, NSEQ]); Bctmp = Buf("ctmp")

    sc.dma("sp", "c0", cF[:], consts_d[:, 0:4, :], writes=[BcF])
    sc.dma("sp", "c1", normgT[:], normgT_d[:, :, :], writes=[Bnormg])
    sc.dma("sp", "c2", badaT[:], badaT_d[:, :, :], writes=[Bbada])
    sc.dma("sp", "c3", cact[:], cT_d[:, :, :], writes=[Bcact])
    sc.op("act", lambda: A.activation(out=ctmp[:], in_=cact[:], func=AF.Tanh, scale=0.5), [Bcact], [Bctmp])
    sc.op("dve", lambda: V.scalar_tensor_tensor(out=ctmp[:], in0=ctmp[:], scalar=1.0, in1=cact[:], op0=ALU.add, op1=ALU.mult), [Bctmp, Bcact], [Bctmp])
    sc.op("dve", lambda: V.tensor_scalar_mul(out=cact[:], in0=ctmp[:], scalar1=0.5), [Bctmp], [Bcact])

    PB = [nc.alloc_psum_tensor("pb%d" % i, [128, 512], F32) for i in range(8)]
    PBb = [Buf("pb%d" % i, excl=True) for i in range(8)]
    PB4b = PB[4][:].bitcast(BF16)

    def pq(bank, c0, c1):
        return [PBb[bank]]

    xres = sb("xres", [128, NT, D]); Bx = [Buf("x%d" % t) for t in range(NT)]
    win = sb("win", [128, 8, DIN], BF16); Bwin = [Buf("win%d" % i) for i in range(8)]

    def load_win(l_):
        for kc_ in range(8):
            sc.dma("pool", "win", win[:, kc_, :], win_d[l_, kc_ * 128:(kc_ + 1) * 128, :], writes=[Bwin[kc_]])

    for t in range(NT):
        sc.dma("sp", "x%d" % t, xres[:, t, :], x_d[0, t * 128:(t + 1) * 128, :], writes=[Bx[t]])
    load_win(0)

    NSLOT = 3
    stg_cm = [nc.sbuf_tensor("stg%d" % i, [128, 8, 512], F32) for i in range(NSLOT)]
    stg = [cm.__enter__() for cm in stg_cm]
    Bstg = [Buf("stg%d" % i) for i in range(NSLOT)]
    cst_v = stg[0][:, 0:4, 0:128]
    sc.dma("sp", "stg0", cst_v, consts_d[:, 4:8, :], writes=[Bstg[0]])
    sc.op("dve", lambda: V.tensor_copy(out=cB[:], in_=cst_v), [Bstg[0]], [BcB])
    slot = 1
    for l in range(DEPTH):
        for pc in range(6):
            s = slot
            slot = (slot + 1) % NSLOT
            sv = stg[s]
            sc.dma("sp", "stg%d" % s, sv[:], wada_d[l, :, pc * 512:(pc + 1) * 512].rearrange("(k p) n -> p k n", p=128), writes=[Bstg[s]])
            for fc in range(4):
                j = pc * 4 + fc
                bank = j % 2
                for kc in range(8):
                    sc.op("pe", lambda: T.matmul(PB[bank][:, 0:NSEQ], lhsT=sv[:, kc, fc * 128:(fc + 1) * 128], rhs=cact[:, kc, :], start=(kc == 0), stop=(kc == 7)),
                          [Bstg[s], Bcact], pq(bank, 0, NSEQ))
                sc.op("dve", lambda: V.tensor_scalar_add(out=modT[:, l, j, :], in0=PB[bank][:, 0:NSEQ], scalar1=badaT[:, l, j:j + 1]),
                      pq(bank, 0, NSEQ) + [Bbada], [BmodT])
    for l in range(DEPTH):
        for b in range(NSEQ):
            sc.op("dve", lambda: V.scalar_tensor_tensor(out=Amod[:, l, b, :], in0=modT[:, l, 8:16, b], scalar=1.0, in1=normgT[:, l, :], op0=ALU.add, op1=ALU.mult),
                  [BmodT, Bnormg], [BAmod])
    sc.barrier()
    if STOP == "pro":
        return nc
    for cm in stg_cm[::-1]:
        cm.__exit__(None, None, None)

    wout = sb("wout", [128, 8, D], BF16); Bwout = Buf("wout")
    kT = sb("kT", [128, 3, S], BF16); BkT = [Buf("kT%d" % t) for t in range(NT)]
    vc = sb("vc", [128, NT, 384], BF16); Bvc = [Buf("vc%d" % t) for t in range(NT)]
    wsTb = sb("wsTb", [128, 4, 128], BF16); BwsTb = Buf("wsTb")
    gainA = sb("gainA", [128, 256]); BgainA = Buf("gainA")
    bsTs = sb("bsTs", [128, 4]); BbsT = Buf("bsT")
    convw = sb("convw", [96, 8, 4]); Bconvw = Buf("convw")
    convb = sb("convb", [96, 8]); Bconvb = Buf("convb")
    gbias = sb("gbias", [128, 8]); Bgbias = Buf("gbias")
    hnorm = sb("hnorm", [128, 384]); Bhnorm = Buf("hnorm")
    qnb = sb("qnb", [128, 64]); Bqnb = Buf("qnb")
    knb = sb("knb", [128, 64]); Bknb = Buf("knb")
    hT = sb("hT", [128, 8, GW], BF16); BhT = [Buf("hT%d" % r) for r in range(GT)]
    qkT = sb("qkT", [96, 8, GW], BF16); BqkT = [Buf("qkT%d" % u) for u in range(8)]
    hist = sb("hist", [96, 8, 3]); Bhist = [Buf("hist%d" % u) for u in range(8)]
    qTa = sb("qTa", [128, 3, GW], BF16); qTb = sb("qTb", [128, 3, GW], BF16)
    BqT = [Buf("qT%d" % r) for r in range(GT)]
    szc = sb("szc", [128, GT, 384], BF16); Bszc = [Buf("szc%d" % r) for r in range(GT)]
    ss = sb("ss", [128, 16]); Bss = Buf("ss")
    vaug = sb("vaug", [128, 4, 97], BF16); Bvaug = Buf("vaug")
    gt = sb("gt", [128, 8]); Bgt = Buf("gt")
    sm = sb("sm", [128, 48]); Bsm = Buf("sm")
    C32 = sb("C32", [96, 4, 97]); BC32 = [Buf("C32_%d" % h) for h in range(4)]
    Cb = sb("Cb", [96, 4, 97], BF16); BCb = [Buf("Cb_%d" % h) for h in range(4)]
    ARENA = 11328
    CH = 64
    arena = sb("arena", [128, ARENA // 4])
    arena_bf = arena[:].bitcast(BF16)
    Bar = [Buf("ar%d" % i) for i in range((ARENA + CH - 1) // CH)]

    def av(off, shape, dt=F32, parts=128):
        n = int(np.prod(shape[1:])) * (4 if dt == F32 else 2)
        assert off % 64 == 0 and off + n <= ARENA
        bufs = Bar[off // CH:(off + n + CH - 1) // CH]
        if dt == F32:
            ap = arena[0:parts, off // 4:(off + n) // 4]
        else:
            ap = arena_bf[0:parts, off // 2:(off + n) // 2]
        if len(shape) == 3:
            ap = ap.rearrange("p (a b) -> p a b", a=shape[1])
        return ap, bufs

    xn, Bxn = av(0, [128, D])
    raw0, Braw0 = av(0, [96, GW + 3], parts=96)
    raw1, Braw1 = av(1088, [96, GW + 3], parts=96)
    raw = [raw0, raw1]; Braw = [Braw0, Braw1]
    accs = [None] * 2; Baccs = [None] * 2; thqs = [None] * 2; Bthqs = [None] * 2
    for i_ in range(2):
        accs[i_], Baccs[i_] = av(2176 + i_ * 2048, [96, GW], parts=96)
        thqs[i_], Bthqs[i_] = av(3200 + i_ * 2048, [96, GW], parts=96)
    wsTf, BwsTf = av(0, [128, 4, 128])
    guv_, Bguv_ = av(0, [128, 512]); zgA_, BzgA_ = av(2048, [128, 256]); tmpA_, BtmpA_ = av(3072, [128, 256])
    w512 = [guv_, zgA_, tmpA_]; Bw512 = [Bguv_, BzgA_, BtmpA_]
    w384 = [None] * 3; Bw384 = [None] * 3
    for i_, o_ in enumerate((4096, 5632, 7168)):
        w384[i_], Bw384[i_] = av(o_, [128, 384])
    sq0, Bsq0 = av(0, [128, 384])
    sq97, Bsq97 = av(0, [128, 4, 97])
    PTm4, BPTm4 = av(0, [128, 4, 128], BF16)
    vp, Bvp = av(1024, [128, 4, 97], BF16)
    ktok, Bktok = av(1856, [128, 384], BF16)
    nd, Bnd = av(7168, [128, 4, 97])
    vn, Bvn = av(8768, [128, 256], BF16)
    ya, Bya = av(9280, [128, 256], BF16)
    yb, Byb = av(9792, [128, 384], BF16)
    qn16, Bqn16 = av(10560, [128, 384], BF16)
    kn16, Bkn16 = av(1536, [128, 384], BF16)
    ebuf = [None] * 4; Be = [None] * 4
    xbuf = [None] * 2; Bxb = [None] * 2; spb = [None] * 2; Bsp = [None] * 2; aTb = [None] * 2; BaT = [None] * 2; Lsum = [None] * 2; BLs = [None] * 2
    for i_ in range(4):
        ebuf[i_], Be[i_] = av(i_ * 1024, [128, GW])
    for i_ in range(2):
        xbuf[i_], Bxb[i_] = av(4096 + i_ * 1024, [128, GW])
        spb[i_], Bsp[i_] = av(6144 + i_ * 512, [128, GW], BF16)
        aTb[i_], BaT[i_] = av(7168 + i_ * 512, [128, GW], BF16)
        Lsum[i_], BLs[i_] = av(8192 + i_ * 512, [128, GW], BF16)
    yc, Byc_all = av(9216, [128, GT, 384], BF16)
    Byc = [[Byc_all for h in range(6)] for r in range(GT)]
    gate_bc, Bgate = av(4096, [128, D])
    gcol, Bgcol = av(8192, [128, 128])
    tmp5 = [None] * 2; Btmp5 = [None] * 2
    for i_ in range(2):
        tmp5[i_], Btmp5[i_] = av(i_ * 2048, [128, 512])
    ycT = hT
    print("sbuf bytes remaining", nc.sbuf_bytes_remaining)

    sc.op("pool", lambda: G.memset(vaug[:], 1.0), [], [Bvaug])
    sc.op("pool", lambda: G.memset(qTa[:], 0.0), [], BqT)
    sc.op("pool", lambda: G.memset(qTb[:], 0.0), [], BqT)

    def proj_tok(r, c0, c1, bank):
        n = c1 - c0
        for kc in range(8):
            sc.op("pe", lambda: T.matmul(PB[bank][:, 0:n], lhsT=hT[:, kc, r * 128:(r + 1) * 128], rhs=win[:, kc, c0:c1], start=(kc == 0), stop=(kc == 7)),
                  [BhT[r], Bwin], pq(bank, 0, n))

    def rstd_small(src_ap, dst_ap, n_inv, reads, writes):
        sc.op("act", lambda: A.activation(out=dst_ap, in_=src_ap, func=AF.Ln, scale=n_inv, bias=EPS), reads, writes)
        sc.op("act", lambda: A.activation(out=dst_ap, in_=dst_ap, func=AF.Exp, scale=-0.5), writes, writes)

    PBbf = {4: PB[4][:].bitcast(BF16), 2: PB[2][:].bitcast(BF16)}
    TRB = [4, 2]
    trn = [0]

    def tr_batch(srcs, reads):
        bank = TRB[trn[0] % 2]
        trn[0] += 1
        off = 0
        for (src, pin, cin) in srcs:
            o_ = off
            sc.op("pe", lambda: T.transpose(PBbf[bank][0:cin, o_:o_ + pin], src, identB[0:pin, 0:pin]), list(reads) + [BcB], pq(bank, 0, 1))
            off += pin
        return bank, PBbf[bank]

    tslot = [0]

    def next_tslot():
        tslot[0] = (tslot[0] + 1) % 8
        return tslot[0]

    def stop_at(tag):
        if STOP == tag:
            raise _Stop()

    win_loaded = {(0, 0): True}
    order = [(b_, l_) for b_ in range(NSEQ) for l_ in range(DEPTH)]

    try:
        for b in range(NSEQ):
          for l in range(DEPTH):
              last = (l == DEPTH - 1)
              if not win_loaded.get((b, l)):
                  load_win(l)
              sc.dma("pool", "wout", wout[:], wout_d[l].rearrange("(k p) n -> p k n", p=128), writes=[Bwout])
              sc.dma("sp", "p0", wsTf[:], wsT_d[l], writes=[BwsTf])
              sc.dma("sp", "p1", gainA[:], gav_d[l:l + 1, :].partition_broadcast(128), writes=[BgainA])
              sc.dma("sp", "p2", bsTs[:], bsT_d[:, l, :], writes=[BbsT])
              sc.dma("sp", "p3", convw[:], convw_d[:, l, :, :], writes=[Bconvw])
              sc.dma("sp", "p4", convb[:], convb_d[:, l, :], writes=[Bconvb])
              sc.dma("sp", "p5", gbias[:], gbias_d[l:l + 1, :].partition_broadcast(128), writes=[Bgbias])
              sc.dma("sp", "p6", hnorm[:], hnorm_d[l:l + 1, :].partition_broadcast(128), writes=[Bhnorm])
              sc.dma("sp", "p7", qnb[:], qnorm_d[l:l + 1, :].partition_broadcast(128), writes=[Bqnb])
              sc.dma("sp", "p8", knb[:], knorm_d[l:l + 1, :].partition_broadcast(128), writes=[Bknb])
              for g in range(4):
                  sc.op("dve", lambda: V.tensor_tensor(out=wsTb[:, g, :], in0=wsTf[:, g, :], in1=triU, op=ALU.mult), [BwsTf, BcF], [BwsTb])
              sc.op("dve", lambda: V.tensor_scalar_mul(out=gainA[:], in0=gainA[:], scalar1=0.5), [BgainA], [BgainA])
              sc.op("dve", lambda: V.tensor_scalar_mul(out=bsTs[:], in0=bsTs[:], scalar1=0.5), [BbsT], [BbsT])
              sc.op("dve", lambda: V.tensor_scalar_mul(out=hnorm[:], in0=hnorm[:], scalar1=0.25), [Bhnorm], [Bhnorm])
              sc.op("dve", lambda: V.tensor_scalar_mul(out=qnb[:], in0=qnb[:], scalar1=0.125), [Bqnb], [Bqnb])
              for c in range(8):
                  bank = c // 4
                  cs = (c % 4) * 128
                  sc.op("dve", lambda: V.tensor_scalar_mul(out=gcol[:], in0=onesF, scalar1=modT[:, l, 16 + c, b:b + 1]), [BcF, BmodT], [Bgcol])
                  sc.op("pe", lambda: T.matmul(PB[bank][:, cs:cs + 128], lhsT=gcol[:], rhs=identF, start=True, stop=True), [Bgcol, BcF], pq(bank, cs, cs + 128))
                  sc.op("act", lambda: A.copy(out=gate_bc[:, c * 128:(c + 1) * 128], in_=PB[bank][:, cs:cs + 128]), pq(bank, cs, cs + 128), [Bgate])
              for kc in range(8):
                  sc.op("pool", lambda: G.tensor_tensor(out=wout[:, kc, :], in0=wout[:, kc, :], in1=gate_bc[:], op=ALU.mult), [Bwout, Bgate], [Bwout])
              sc.op("pool", lambda: G.memset(C32[:], 0.0), [], BC32)
              sc.op("pool", lambda: G.memset(Cb[:], 0.0), [], BCb)
              sc.op("pool", lambda: G.memset(hist[:], 0.0), [], Bhist)

              stop_at('g0')
              for I in range(NG):
                  tiles = [I * GT + r for r in range(GT)]
                  for r, t in enumerate(tiles):
                      sc.op("dve", lambda: V.memset(ss[:, 0:1], 0.0), [], [Bss])
                      sc.op("act", lambda: A.activation(out=xn[:], in_=xres[:, t, :], func=AF.Square, accum_out=ss[:, 0:1]), [Bx[t], Bss], [Bxn, Bss], noinline=True)
                      stop_at('g1a')
                      rstd_small(ss[:, 0:1], ss[:, 1:2], 1.0 / D, [Bss], [Bss])
                      stop_at('g1b')
                      sc.op("dve", lambda: V.tensor_scalar_mul(out=xn[:], in0=xres[:, t, :], scalar1=ss[:, 1:2]), [Bx[t], Bss], [Bxn])
                      stop_at('g1c')
                      for c in range(8):
                          bank = 2 + c % 2
                          cs = (c // 2) * 128
                          sc.op("pe", lambda: T.transpose(PB[bank][:, cs:cs + 128], xn[:, c * 128:(c + 1) * 128], identF), [Bxn, BcF], pq(bank, cs, cs + 128))
                          stop_at('g1d')
                          if c % 2 == 0:
                              sc.op("dve", lambda: V.tensor_scalar(out=hT[:, c, r * 128:(r + 1) * 128], in0=PB[bank][:, cs:cs + 128],
                                                                   scalar1=Amod[:, l, b, c:c + 1], scalar2=modT[:, l, c, b:b + 1], op0=ALU.mult, op1=ALU.add),
                                    pq(bank, cs, cs + 128) + [BAmod, BmodT], [BhT[r]])
                              stop_at('g1e')
                          else:
                              sc.op("act", lambda: A.activation(out=hT[:, c, r * 128:(r + 1) * 128], in_=PB[bank][:, cs:cs + 128], func=AF.Identity,
                                                                scale=Amod[:, l, b, c:c + 1], bias=modT[:, l, c, b:b + 1]),
                                    pq(bank, cs, cs + 128) + [BAmod, BmodT], [BhT[r]])
                              if (r, c) == tuple(int(v) for v in os.environ.get("KRC", "0,1").split(",")):
                                  stop_at('g1f')
                  stop_at('g1')
                  def unit_front(u):
                      col0 = 768 + u * 96
                      rb = u % 2
                      ub = 2 + u % 2
                      acc, Bacc = accs[rb], Baccs[rb]
                      for kc in range(8):
                          sc.op("pe", lambda: T.matmul(PB[ub][0:96, 0:GW], lhsT=win[:, kc, col0:col0 + 96], rhs=hT[:, kc, :], start=(kc == 0), stop=(kc == 7)),
                                BhT + [Bwin], pq(ub, 0, GW))
                      sc.op("pool", lambda: G.tensor_copy(out=raw[rb][:, 0:3], in_=hist[:, u, :]), [Bhist[u]], [Braw[rb]])
                      sc.op("act", lambda: A.copy(out=raw[rb][:, 3:3 + GW], in_=PB[ub][0:96, 0:GW]), pq(ub, 0, GW), [Braw[rb]])
                      sc.op("pool", lambda: G.tensor_copy(out=hist[:, u, :], in_=raw[rb][:, GW:GW + 3]), [Braw[rb]], [Bhist[u]])
                      sc.op("act", lambda: A.activation(out=acc[:], in_=raw[rb][:, 3:3 + GW], func=AF.Identity, scale=convw[:, u, 3:4], bias=convb[:, u:u + 1]),
                            [Braw[rb], Bconvw, Bconvb], [Bacc])

                  def unit_back(u):
                      rb = u % 2
                      acc, Bacc, thq, Bthq = accs[rb], Baccs[rb], thqs[rb], Bthqs[rb]
                      for tap in (2, 1, 0):
                          sc.op("dve", lambda: V.scalar_tensor_tensor(out=acc[:], in0=raw[rb][:, tap:tap + GW], scalar=convw[:, u, tap:tap + 1], in1=acc[:], op0=ALU.mult, op1=ALU.add),
                                [Braw[rb], Bconvw, Bacc], [Bacc])
                      sc.op("act", lambda: A.activation(out=thq[:], in_=acc[:], func=AF.Tanh, scale=0.5), [Bacc], [Bthq])
                      sc.op("dve", lambda: V.scalar_tensor_tensor(out=qkT[:, u, :], in0=thq[:], scalar=1.0, in1=acc[:], op0=ALU.add, op1=ALU.mult), [Bthq, Bacc], [BqkT[u]])

                  unit_front(0)
                  for u in range(8):
                      if u + 1 < 8:
                          unit_front(u + 1)
                      unit_back(u)
                  stop_at('g2')
                  for r, t in enumerate(tiles):
                      rc = slice(r * 128, (r + 1) * 128)
                      guv, zgA, tmpA = w512[0], w512[1], w512[2]
                      proj_tok(r, 0, 512, 0)
                      sc.op("act", lambda: A.activation(out=guv[:], in_=PB[0][:, 0:512], func=AF.Gelu_apprx_tanh), pq(0, 0, 512), [Bw512[0]])
                      proj_tok(r, 512, 768, 1)
                      sc.op("act", lambda: A.activation(out=tmpA[:, 0:256], in_=PB[1][:, 0:256], func=AF.Tanh, scale=0.5), pq(1, 0, 256), [Bw512[2]])
                      sc.op("dve", lambda: V.scalar_tensor_tensor(out=zgA[:, 0:256], in0=tmpA[:, 0:256], scalar=1.0, in1=PB[1][:, 0:256], op0=ALU.add, op1=ALU.mult),
                            [Bw512[2]] + pq(1, 0, 256), [Bw512[1]])
                      sc.op("dve", lambda: V.tensor_tensor(out=zgA[:, 0:256], in0=zgA[:, 0:256], in1=guv[:, 0:256], op=ALU.mult), [Bw512[1], Bw512[0]], [Bw512[1]])
                      to_, ozg, tzb = w384[0], w384[1], w384[2]
                      proj_tok(r, 1920, 2304, 0)
                      sc.op("act", lambda: A.activation(out=to_[:], in_=PB[0][:, 0:384], func=AF.Tanh, scale=0.5), pq(0, 0, 384), [Bw384[0]])
                      proj_tok(r, 2304, 2696, 1)
                      sc.op("act", lambda: A.activation(out=tzb[:], in_=PB[1][:, 0:384], func=AF.Tanh, scale=0.5), pq(1, 0, 384), [Bw384[2]])
                      sc.op("dve", lambda: V.scalar_tensor_tensor(out=ozg[:], in0=tzb[:], scalar=1.0, in1=PB[1][:, 0:384], op0=ALU.add, op1=ALU.mult),
                            [Bw384[2]] + pq(1, 0, 384), [Bw384[1]])
                      sc.op("dve", lambda: V.scalar_tensor_tensor(out=ozg[:], in0=to_[:], scalar=1.0, in1=ozg[:], op0=ALU.add, op1=ALU.mult), [Bw384[0], Bw384[1]], [Bw384[1]])
                      sc.op("pool", lambda: G.tensor_tensor(out=ozg[:], in0=ozg[:], in1=hnorm[:], op=ALU.mult), [Bw384[1], Bhnorm], [Bw384[1]])
                      sc.op("dve", lambda: V.tensor_tensor(out=gt[:], in0=PB[1][:, 384:392], in1=gbias[:], op=ALU.add), pq(1, 384, 392) + [Bgbias], [Bgt])
                      proj_tok(r, 3848, 4232, 0)
                      sc.op("act", lambda: A.activation(out=to_[:], in_=PB[0][:, 0:384], func=AF.Tanh, scale=0.5), pq(0, 0, 384), [Bw384[0]])
                      sc.op("dve", lambda: V.scalar_tensor_tensor(out=szc[:, r, :], in0=to_[:], scalar=1.0, in1=PB[0][:, 0:384], op0=ALU.add, op1=ALU.mult),
                            [Bw384[0]] + pq(0, 0, 384), [Bszc[r]])
                      proj_tok(r, 1536, 1920, 1)
                      sc.op("act", lambda: A.copy(out=vaug[:, :, 0:96], in_=PB[1][:, 0:384].rearrange("p (h d) -> p h d", h=4)), pq(1, 0, 384), [Bvaug])
                      proj_tok(r, 3464, 3848, 0)
                      sc.op("act", lambda: A.copy(out=vc[:, t, :], in_=PB[0][:, 0:384]), pq(0, 0, 384), [Bvc[t]])
                      proj_tok(r, 2696, 3080, 1)
                      proj_tok(r, 3080, 3464, 0)
                      sc.op("dve", lambda: V.tensor_tensor(out=tmpA[:, 0:256], in0=guv[:, 256:512], in1=guv[:, 256:512], op=ALU.mult), [Bw512[0]], [Bw512[2]])
                      sc.op("dve", lambda: V.reduce_sum(out=ss[:, 2:6], in_=tmpA[:, 0:256].rearrange("p (g d) -> p g d", g=4), axis=AX.X), [Bw512[2]], [Bss])
                      rstd_small(ss[:, 2:6], ss[:, 2:6], 1.0 / 64, [Bss], [Bss])
                      sc.op("dve", lambda: V.tensor_tensor(out=tmpA[:, 0:256].rearrange("p (g d) -> p g d", g=4), in0=guv[:, 256:512].rearrange("p (g d) -> p g d", g=4),
                                                           in1=ss[:, 2:6].unsqueeze(2).to_broadcast([128, 4, 64]), op=ALU.mult), [Bw512[0], Bss], [Bw512[2]])
                      sc.op("dve", lambda: V.tensor_tensor(out=vn[:], in0=tmpA[:, 0:256], in1=gainA[:], op=ALU.mult), [Bw512[2], BgainA], [Bvn])
                      for g in range(4):
                          sc.op("pe", lambda: T.matmul(PB[7][:, g * 64:(g + 1) * 64], lhsT=wsTb[:, g, :], rhs=vn[:, g * 64:(g + 1) * 64], start=True, stop=True),
                                [BwsTb, Bvn], pq(7, g * 64, g * 64 + 64))
                      for g in range(4):
                          sc.op("dve", lambda: V.scalar_tensor_tensor(out=ya[:, g * 64:(g + 1) * 64], in0=PB[7][:, g * 64:(g + 1) * 64], scalar=bsTs[:, g:g + 1],
                                                                      in1=zgA[:, g * 64:(g + 1) * 64], op0=ALU.add, op1=ALU.mult),
                                pq(7, g * 64, g * 64 + 64) + [BbsT, Bw512[1]], [Bya])
                      sc.op("act", lambda: A.activation(out=sm[:, 0:4], in_=gt[:, 4:8], func=AF.Exp, scale=-1.0), [Bgt], [Bsm])
                      sc.op("act", lambda: A.activation(out=sm[:, 0:4], in_=sm[:, 0:4], func=AF.Ln, bias=1.0), [Bsm], [Bsm])
                      sc.op("pe", lambda: T.matmul(PB[6][:, 400:404], lhsT=triU, rhs=sm[:, 0:4], start=True, stop=True), [BcF, Bsm], pq(6, 400, 404))
                      sc.op("pe", lambda: T.matmul(PB[6][:, 404:408], lhsT=onesF, rhs=sm[:, 0:4], start=True, stop=True), [BcF, Bsm], pq(6, 404, 408))
                      sc.op("dve", lambda: V.tensor_copy(out=sm[:, 8:16], in_=PB[6][:, 400:408]), pq(6, 400, 408), [Bsm])
                      sc.op("dve", lambda: V.tensor_tensor(out=sm[:, 16:20], in0=gt[:, 0:4], in1=sm[:, 8:12], op=ALU.add), [Bgt, Bsm], [Bsm])
                      sc.op("act", lambda: A.activation(out=sm[:, 16:20], in_=sm[:, 16:20], func=AF.Exp), [Bsm], [Bsm])
                      sc.op("act", lambda: A.activation(out=sm[:, 24:32], in_=sm[:, 8:16], func=AF.Exp, scale=-1.0), [Bsm], [Bsm])
                      sc.op("dve", lambda: V.tensor_tensor(out=vp[:], in0=vaug[:], in1=sm[:, 16:20].unsqueeze(2).to_broadcast([128, 4, 97]), op=ALU.mult), [Bvaug, Bsm], [Bvp])
                      for h in range(4):
                          sc.op("pe", lambda: T.matmul(PB[3][:, h * 128:(h + 1) * 128], lhsT=qkT[:, 4 + h, rc], rhs=qkT[:, h, rc], start=True, stop=True),
                                [BqkT[4 + h], BqkT[h]], pq(3, 0, 512))
                      sc.op("dve", lambda: V.tensor_tensor(out=PTm4[:], in0=PB[3][:, 0:512].rearrange("p (h t) -> p h t", h=4),
                                                           in1=triU.unsqueeze(1).to_broadcast([128, 4, 128]), op=ALU.mult), pq(3, 0, 512) + [BcF], [BPTm4])
                      for h in range(4):
                          hv = slice(h * 97, (h + 1) * 97)
                          sc.op("pe", lambda: T.matmul(PB[5][:, hv], lhsT=PTm4[:, h, :], rhs=vp[:, h, :], start=True, stop=False), [BPTm4, Bvp], pq(5, 0, 388))
                          sc.op("pe", lambda: T.matmul(PB[5][:, hv], lhsT=qkT[:, h, rc], rhs=Cb[:, h, :], start=False, stop=True), [BqkT[h]] + BCb, pq(5, 0, 388))
                      kb, kps = tr_batch([(qkT[:, 4 + h, rc], 96, 128) for h in range(4)], [BqkT[4 + h] for h in range(4)])
                      sc.op("act", lambda: A.copy(out=ktok[:], in_=kps[:, 0:384]), pq(kb, 0, 1), [Bktok])
                      for h in range(4):
                          hv = slice(h * 97, (h + 1) * 97)
                          sc.op("pe", lambda: T.matmul(PB[7][0:96, hv], lhsT=ktok[:, h * 96:(h + 1) * 96], rhs=vp[:, h, :], start=True, stop=True), [Bktok, Bvp], pq(7, 0, 388))
                      sc.op("dve", lambda: V.tensor_tensor(out=nd[:], in0=PB[5][:, 0:388].rearrange("p (h d) -> p h d", h=4),
                                                           in1=sm[:, 24:28].unsqueeze(2).to_broadcast([128, 4, 97]), op=ALU.mult), pq(5, 0, 388) + [Bsm], [Bnd])
                      sc.op("dve", lambda: V.scalar_tensor_tensor(out=sm[:, 32:36], in0=nd[:, :, 96], scalar=-1.0, in1=nd[:, :, 96], op0=ALU.mult, op1=ALU.max), [Bnd], [Bsm])
                      sc.op("dve", lambda: V.tensor_scalar_max(out=sm[:, 32:36], in0=sm[:, 32:36], scalar1=CS_INV), [Bsm], [Bsm])
                      sc.op("dve", lambda: V.reciprocal(out=sm[:, 32:36], in_=sm[:, 32:36]), [Bsm], [Bsm])
                      sc.op("dve", lambda: V.tensor_tensor(out=sq97[:], in0=nd[:], in1=nd[:], op=ALU.mult), [Bnd], [Bsq97])
                      sc.op("dve", lambda: V.reduce_sum(out=sm[:, 36:40], in_=sq97[:, :, 0:96], axis=AX.X), [Bsq97], [Bsm])
                      sc.op("dve", lambda: V.tensor_tensor(out=sm[:, 40:44], in0=sm[:, 32:36], in1=sm[:, 32:36], op=ALU.mult), [Bsm], [Bsm])
                      sc.op("dve", lambda: V.tensor_tensor(out=sm[:, 40:44], in0=sm[:, 40:44], in1=sm[:, 36:40], op=ALU.mult), [Bsm], [Bsm])
                      rstd_small(sm[:, 40:44], sm[:, 40:44], 1.0 / 96, [Bsm], [Bsm])
                      sc.op("dve", lambda: V.tensor_tensor(out=sm[:, 44:48], in0=sm[:, 40:44], in1=sm[:, 32:36], op=ALU.mult), [Bsm], [Bsm])
                      for h in range(4):
                          sc.op("dve", lambda: V.scalar_tensor_tensor(out=yb[:, h * 96:(h + 1) * 96], in0=nd[:, h, 0:96], scalar=sm[:, 44 + h:45 + h], in1=ozg[:, h * 96:(h + 1) * 96],
                                                                      op0=ALU.mult, op1=ALU.mult), [Bnd, Bsm, Bw384[1]], [Byb])
                      sc.op("dve", lambda: V.tensor_tensor(out=C32[:], in0=C32[:], in1=PB[7][0:96, 0:388].rearrange("p (h d) -> p h d", h=4), op=ALU.add),
                            BC32 + pq(7, 0, 388), BC32)
                      sc.op("pool", lambda: G.tensor_tensor(out=C32[:], in0=C32[:], in1=sm[0:96, 28:32].unsqueeze(2).to_broadcast([96, 4, 97]), op=ALU.mult), BC32 + [Bsm], BC32)
                      sc.op("pool", lambda: G.tensor_copy(out=Cb[:], in_=C32[:]), BC32, BCb)
                      cbuf = [(1, sq0, Bsq0, w384[0], Bw384[0], qn16, Bqn16, qnb, Bqnb, 2), (0, w384[1], Bw384[1], w384[2], Bw384[2], kn16, Bkn16, knb, Bknb, 8)]
                      for (pb_, sqx, Bsqx, fx, Bfx, n16, Bn16, gsrc, Bg, so) in cbuf:
                          sc.op("act", lambda: A.activation(out=sqx[:], in_=PB[pb_][:, 0:384], func=AF.Square), pq(pb_, 0, 384), [Bsqx])
                      for (pb_, sqx, Bsqx, fx, Bfx, n16, Bn16, gsrc, Bg, so) in cbuf:
                          sc.op("dve", lambda: V.reduce_sum(out=ss[:, so:so + 6], in_=sqx[:].rearrange("p (h d) -> p h d", h=6), axis=AX.X), [Bsqx], [Bss])
                      rstd_small(ss[:, 2:14], ss[:, 2:14], 1.0 / 64, [Bss], [Bss])
                      for (pb_, sqx, Bsqx, fx, Bfx, n16, Bn16, gsrc, Bg, so) in cbuf:
                          sc.op("dve", lambda: V.tensor_tensor(out=fx[:].rearrange("p (h d) -> p h d", h=6), in0=PB[pb_][:, 0:384].rearrange("p (h d) -> p h d", h=6),
                                                               in1=ss[:, so:so + 6].unsqueeze(2).to_broadcast([128, 6, 64]), op=ALU.mult), pq(pb_, 0, 384) + [Bss], [Bfx])
                      for (pb_, sqx, Bsqx, fx, Bfx, n16, Bn16, gsrc, Bg, so) in cbuf:
                          sc.op("dve" if pb_ else "pool", (lambda: V.tensor_tensor(out=n16[:].rearrange("p (h d) -> p h d", h=6), in0=fx[:].rearrange("p (h d) -> p h d", h=6),
                                                                                    in1=gsrc[:].unsqueeze(1).to_broadcast([128, 6, 64]), op=ALU.mult)) if pb_ else
                                (lambda: G.tensor_tensor(out=n16[:].rearrange("p (h d) -> p h d", h=6), in0=fx[:].rearrange("p (h d) -> p h d", h=6),
                                                         in1=gsrc[:].unsqueeze(1).to_broadcast([128, 6, 64]), op=ALU.mult)), [Bfx, Bg], [Bn16])
                      tb, tps_ = tr_batch([(qn16[:, p * 128:(p + 1) * 128], 128, 128) for p in range(3)], [Bqn16])
                      sc.op("dve", lambda: V.tensor_copy(out=qTa[0:64, :, rc], in_=tps_[0:64, 0:384].rearrange("p (c t) -> p c t", c=3)), pq(tb, 0, 1), [BqT[r]])
                      sc.op("act", lambda: A.copy(out=qTb[64:128, :, rc], in_=tps_[64:128, 0:384].rearrange("p (c t) -> p c t", c=3)), pq(tb, 0, 1), [BqT[r]])
                      tb, tps_ = tr_batch([(kn16[:, p * 128:(p + 1) * 128], 128, 128) for p in range(3)], [Bkn16])
                      sc.op("act", lambda: A.copy(out=kT[:, :, t * 128:(t + 1) * 128], in_=tps_[:, 0:384].rearrange("p (c t) -> p c t", c=3)), pq(tb, 0, 1), [BkT[t]])
                      tb, tps_ = tr_batch([(ya[:, j * 128:(j + 1) * 128], 128, 128) for j in range(2)] + [(yb[:, j * 128:(j + 1) * 128], 128, 128) for j in range(3)], [Bya, Byb])
                      sc.op("dve", lambda: V.tensor_copy(out=ycT[:, 0:5, rc], in_=tps_[:, 0:640].rearrange("p (c t) -> p c t", c=5)), pq(tb, 0, 1), [BhT[r]])
                  stop_at('g3')
                  if I == NG - 1:
                      nxt = order.index((b, l)) + 1
                      if nxt < len(order):
                          load_win(order[nxt][1])
                          win_loaded[order[nxt]] = True
                  jmax = I * GT + GT - 1
                  its = []
                  for hp in range(3):
                      for j in range(jmax, -1, -1):
                          for e_ in range(2):
                              its.append((hp * 2 + e_, j))
                  ZB = [5, 3, 0]
                  CBk = [6, 2]
                  nit = len(its)

                  def geo(n):
                      hd, j = its[n]
                      r0 = max(0, j - I * GT)
                      return hd, j, hd // 2, hd % 2, r0, r0 * 128, GW - r0 * 128

                  for e_ in range(2):
                      sc.op("dve", lambda: V.memset(PB[7][:, e_ * 128:(e_ + 1) * 128], 0.0), [], pq(7, 0, 128))
                  for k in range(nit + 5):
                      if k < nit:
                          hd, j, p, e_, r0, t0, N = geo(k)
                          diag = j >= I * GT
                          qs = (qTa if e_ == 0 else qTb)[:, p, t0:GW]
                          zb = ZB[k % 3]
                          sc.op("pe", lambda: T.matmul(PB[zb][:, 0:N], lhsT=kT[:, p, j * 128:(j + 1) * 128], rhs=qs, start=True, stop=not diag), [BkT[j]] + BqT, pq(zb, 0, N))
                          if diag:
                              sc.op("pe", lambda: T.matmul(PB[zb][:, 0:128], lhsT=identB, rhs=nmaskS, start=False, stop=True), [BcB], pq(zb, 0, 128))
                      n = k - 1
                      if 0 <= n < nit:
                          hd, j, p, e_, r0, t0, N = geo(n)
                          zb = ZB[n % 3]
                          sc.op("act", lambda: A.activation(out=ebuf[n % 4][:, 0:N], in_=PB[zb][:, 0:N], func=AF.Exp), pq(zb, 0, N), [Be[n % 4]])
                      n = k - 2
                      if 0 <= n < nit:
                          hd, j, p, e_, r0, t0, N = geo(n)
                          cb = CBk[n % 2]
                          carry = j < jmax
                          sc.op("pe", lambda: T.matmul(PB[cb][:, 0:N], lhsT=nTriL, rhs=spb[n % 2][:, 0:N], start=True, stop=not carry), [BcB, Bsp[n % 2]], pq(cb, 0, N))
                          if carry:
                              sc.op("pe", lambda: T.matmul(PB[cb][:, 0:N], lhsT=nOnes, rhs=Lsum[e_][:, t0:GW], start=False, stop=True), [BcB, BLs[e_]], pq(cb, 0, N))
                          if j == jmax:
                              sc.op("pool", lambda: G.memset(Lsum[e_][:], 0.0), [], [BLs[e_]])
                          if j > 0:
                              sc.op("pool", lambda: G.tensor_tensor(out=Lsum[e_][:, t0:GW], in0=Lsum[e_][:, t0:GW], in1=spb[n % 2][:, 0:N], op=ALU.add), [BLs[e_], Bsp[n % 2]], [BLs[e_]])
                      n = k - 3
                      if 0 <= n < nit:
                          hd, j, p, e_, r0, t0, N = geo(n)
                          cb = CBk[n % 2]
                          sc.op("act", lambda: A.activation(out=xbuf[n % 2][:, 0:N], in_=PB[cb][:, 0:N], func=AF.Exp), pq(cb, 0, N), [Bxb[n % 2]])
                          sc.op("dve", lambda: V.tensor_tensor(out=aTb[n % 2][:, 0:N], in0=xbuf[n % 2][:, 0:N], in1=ebuf[n % 4][:, 0:N], op=ALU.mult), [Bxb[n % 2], Be[n % 4]], [BaT[n % 2]])
                      n = k - 1
                      if 0 <= n < nit:
                          hd, j, p, e_, r0, t0, N = geo(n)
                          sc.op("act", lambda: A.activation(out=spb[n % 2][:, 0:N], in_=ebuf[n % 4][:, 0:N], func=AF.Ln, bias=1.0), [Be[n % 4]], [Bsp[n % 2]])
                      n = k - 4
                      if 0 <= n < nit:
                          hd, j, p, e_, r0, t0, N = geo(n)
                          for rr in range(r0, GT):
                              oc = e_ * 128 + rr * 64
                              sc.op("pe", lambda: T.matmul(PB[7][:, oc:oc + 64], lhsT=aTb[n % 2][:, (rr - r0) * 128:(rr - r0 + 1) * 128], rhs=vc[:, j, hd * 64:(hd + 1) * 64],
                                                           start=False, stop=(j == 0), skip_group_check=True), [BaT[n % 2], Bvc[j]], pq(7, oc, oc + 64))
                          if j == 0:
                              for rr in range(GT):
                                  oc = e_ * 128 + rr * 64
                                  sc.op("dve", lambda: V.scalar_tensor_tensor(out=yc[:, rr, hd * 64:(hd + 1) * 64], in0=PB[7][:, oc:oc + 64], scalar=0.5,
                                                                              in1=szc[:, rr, hd * 64:(hd + 1) * 64], op0=ALU.mult, op1=ALU.mult),
                                        pq(7, oc, oc + 64) + [Bszc[rr]], [Byc[rr][hd]])
                              if hd < 4:
                                  sc.op("dve", lambda: V.memset(PB[7][:, e_ * 128:(e_ + 1) * 128], 0.0), [], pq(7, 0, 128))
                  stop_at('g4')
                  for r, t in enumerate(tiles):
                      rc = slice(r * 128, (r + 1) * 128)
                      tb, tps_ = tr_batch([(yc[:, r, j * 128:(j + 1) * 128], 128, 128) for j in range(3)], [Byc_all])
                      sc.op("act", lambda: A.copy(out=ycT[:, 5:8, rc], in_=tps_[:, 0:384].rearrange("p (c t) -> p c t", c=3)), pq(tb, 0, 1), [BhT[r]])
                      if debug and b == 0 and l == 0 and I == 0 and r == GT - 1:
                          for c in range(8):
                              sc.op("dve", lambda: V.tensor_copy(out=tmp5[0][:, 0:GW], in_=ycT[:, c, :]), BhT, [Btmp5[0]])
                              sc.dma("sp", "dbg", dbg_d[:, c, :], tmp5[0][:, 0:GW], reads=[Btmp5[0]])
                      for half in range(2):
                          bank = half
                          hs = slice(half * 512, (half + 1) * 512)
                          for kc in range(8):
                              sc.op("pe", lambda: T.matmul(PB[bank][:, 0:512], lhsT=ycT[:, kc, rc], rhs=wout[:, kc, hs], start=(kc == 0), stop=(kc == 7)),
                                    [BhT[r], Bwout], pq(bank, 0, 512))
                          sc.op("dve", lambda: V.tensor_tensor(out=xres[:, t, hs], in0=PB[bank][:, 0:512], in1=xres[:, t, hs], op=ALU.add), pq(bank, 0, 512) + [Bx[t]], [Bx[t]])
                      if last:
                          sc.dma("sp", "o%d" % t, y_d[b, t * 128:(t + 1) * 128, :], xres[:, t, :], reads=[Bx[t]])
                          if b + 1 < NSEQ:
                              sc.dma("sp", "x%d" % t, xres[:, t, :], x_d[b + 1, t * 128:(t + 1) * 128, :], writes=[Bx[t]])
    except _Stop:
        pass
    sc.barrier()
    print("instructions", sc.ninst, "waits", sc.nwaits)
    return nc


def host_inputs(inp, core, NSEQ, DEPTH):
    f = lambda a: np.ascontiguousarray(np.asarray(a, dtype=np.float32))
    b0 = core * NSEQ
    m = {}
    m["x"] = f(inp["x"][b0:b0 + NSEQ])
    c = np.asarray(inp["c"], np.float32)[b0:b0 + NSEQ]
    m["cT"] = f(c.reshape(NSEQ, 8, 128).transpose(2, 1, 0))
    m["w_ada"] = f(inp["w_ada"][:DEPTH])
    m["b_adaT"] = f(np.asarray(inp["b_ada"], np.float32)[:DEPTH].reshape(DEPTH, 24, 128).transpose(2, 0, 1))
    m["normgT"] = f(np.asarray(inp["norm_g"], np.float32)[:DEPTH].reshape(DEPTH, 8, 128).transpose(2, 0, 1))
    m["w_in"] = f(inp["w_in"][:DEPTH])
    m["w_out"] = f(inp["w_out"][:DEPTH])
    m["gav"] = f(np.asarray(inp["ga_v_norm"], np.float32)[:DEPTH].reshape(DEPTH, 256))
    m["wsT"] = f(np.asarray(inp["ga_ws"], np.float32)[:DEPTH].transpose(0, 3, 1, 2))
    m["bsT"] = f(np.asarray(inp["ga_bs"], np.float32)[:DEPTH].transpose(2, 0, 1))
    cw = np.asarray(inp["mb_conv_w"], np.float32)[:DEPTH]
    m["convw"] = f(cw.reshape(DEPTH, 4, 8, 96).transpose(3, 0, 2, 1))
    m["convb"] = f(np.asarray(inp["mb_conv_b"], np.float32)[:DEPTH].reshape(DEPTH, 8, 96).transpose(2, 0, 1))
    m["gbias"] = f(np.concatenate([np.asarray(inp["mb_b_i"], np.float32)[:DEPTH], np.asarray(inp["mb_b_f"], np.float32)[:DEPTH]], axis=1))
    m["hnorm"] = f(np.asarray(inp["mb_h_norm"], np.float32)[:DEPTH].reshape(DEPTH, 384))
    m["qnorm"] = f(np.asarray(inp["sc_q_norm"], np.float32)[:DEPTH])
    m["knorm"] = f(np.asarray(inp["sc_k_norm"], np.float32)[:DEPTH])
    i = np.arange(128)
    cs = np.zeros((128, 8, 128), np.float32)
    cs[:, 0] = np.eye(128)
    cs[:, 1] = (i[:, None] <= i[None, :])
    cs[:, 2] = np.where(i[:, None] <= i[None, :], 0.0, BIG)
    cs[:, 3] = 1.0
    cs[:, 4] = np.eye(128)
    cs[:, 5] = -(i[:, None] >= i[None, :]).astype(np.float32)
    cs[:, 6] = np.where(i[:, None] < i[None, :], 0.0, -BIG)
    cs[:, 7] = -1.0
    m["consts"] = cs
    return m


_NC_CACHE = {}


def run(inp, NSEQ, S, DEPTH, ncores, debug=False):
    key = (NSEQ, S, DEPTH, debug)
    if key not in _NC_CACHE:
        _NC_CACHE[key] = build(NSEQ, S, DEPTH, debug)
    nc = _NC_CACHE[key]
    in_maps = [host_inputs(inp, c, NSEQ, DEPTH) for c in range(ncores)]
    res = run_bass_kernel_spmd(nc, in_maps, core_ids=list(range(ncores)))
    out = np.concatenate([r["y"] for r in res.results], axis=0)
    if debug:
        return out, res.results[0]["dbg"]
    return out


def kernel(**inputs):
    out = run(inputs, 4, 2048, 2, 8)
    return np.ascontiguousarray(out.astype(np.float32))
```

```python
import numpy as np
import concourse.bass as bass
import concourse.mybir as mybir
from concourse.bass_utils import run_bass_kernel_spmd

F32 = mybir.dt.float32
BF16 = mybir.dt.bfloat16
AF = mybir.ActivationFunctionType
ALU = mybir.AluOpType
AX = mybir.AxisListType

D = 1024
DIN = 4232
EPS = 1e-6
CS_INV = 4.0 * (96.0 ** 0.5)
BIG = 30000.0


class Buf:
    __slots__ = ("name", "w", "r", "excl")

    def __init__(self, name, excl=False):
        self.name = name
        self.w = {}
        self.r = {}
        self.excl = excl


class Sched:
    def __init__(self, nc):
        self.nc = nc
        self.eng = {"pe": nc.tensor, "act": nc.scalar, "dve": nc.vector, "pool": nc.gpsimd, "sp": nc.sync}
        self.sems = {}
        self.cnt = {}
        self.known = {e: {} for e in self.eng}
        self._ctx = []
        for e in ["pe", "act", "dve", "pool"]:
            self._mksem(e)
        self.ninst = 0
        self.nwaits = 0

    def _mksem(self, key):
        cm = self.nc.semaphore("s_" + key)
        s = cm.__enter__()
        self._ctx.append(cm)
        self.sems[key] = s
        self.cnt[key] = 0

    def _wait(self, e, k, v):
        if self.known[e].get(k, 0) >= v:
            return
        self.eng[e].wait_ge(self.sems[k], v)
        self.known[e][k] = v
        self.nwaits += 1

    def _need(self, e, reads, writes):
        need = {}
        for b in reads:
            for k, v in b.w.items():
                if need.get(k, 0) < v:
                    need[k] = v
        for b in writes:
            for k, v in b.w.items():
                if k != e and need.get(k, 0) < v:
                    need[k] = v
            for k, v in b.r.items():
                if k != e and need.get(k, 0) < v:
                    need[k] = v
        return [(k, v) for k, v in need.items() if self.known[e].get(k, 0) < v]

    def _deps(self, e, reads, writes):
        for k, v in self._need(e, reads, writes):
            self._wait(e, k, v)

    def _record(self, k, v, reads, writes):
        for b in reads:
            if b.r.get(k, 0) < v:
                b.r[k] = v
        for b in writes:
            b.w[k] = v
            b.r = {}

    @staticmethod
    def _flat(lst):
        out = []
        for b in lst:
            if isinstance(b, (list, tuple)):
                out.extend(b)
            else:
                out.append(b)
        return out

    def op(self, e, fn, reads=(), writes=(), noinline=False, pe_inline=False):
        reads, writes = self._flat(reads), self._flat(writes)
        ex = [b for b in reads if b.excl]
        if ex:
            reads = [b for b in reads if not b.excl]
            writes = list(writes) + ex
        need = self._need(e, reads, writes)
        inline = None
        if need and ((e in ("dve", "pool", "act") and not noinline) or (e == "pe" and pe_inline)):
            inline = need.pop()
        for k, v in need:
            self._wait(e, k, v)
        ins = fn()
        if inline is not None:
            ins._wait_ge(self.sems[inline[0]], inline[1])
            self.known[e][inline[0]] = inline[1]
        self.cnt[e] += 1
        ins.then_inc(self.sems[e], 1)
        self._record(e, self.cnt[e], reads, writes)
        self.ninst += 1

    def dma(self, q, key, out, in_, reads=(), writes=()):
        if key not in self.sems:
            self._mksem(key)
        reads, writes = self._flat(reads), self._flat(writes)
        self._deps(q, reads, writes)
        ins = self.eng[q].dma_start(out=out, in_=in_)
        self.cnt[key] += 16
        ins.then_inc(self.sems[key], 16)
        self._record(key, self.cnt[key], reads, writes)
        self.ninst += 1

    def barrier(self):
        for e in self.eng:
            for k, v in self.cnt.items():
                if v > 0:
                    self._wait(e, k, v)


class _Stop(Exception):
    pass


def build(NSEQ, S, DEPTH, debug=False):
    import os
    STOP = os.environ.get("KSTOP", "")
    NT = S // 128
    GT = 2
    GW = GT * 128
    NG = NT // GT
    nc = bass.Bass("TRN2", target_bir_lowering=False)
    sc = Sched(nc)
    V, A, G, T = nc.vector, nc.scalar, nc.gpsimd, nc.tensor

    def dram(name, shape, dt=F32, kind="ExternalInput"):
        return nc.dram_tensor(name, list(shape), dt, kind=kind).ap()

    x_d = dram("x", [NSEQ, S, D])
    cT_d = dram("cT", [128, 8, NSEQ])
    wada_d = dram("w_ada", [DEPTH, D, 3 * D])
    badaT_d = dram("b_adaT", [128, DEPTH, 24])
    normgT_d = dram("normgT", [128, DEPTH, 8])
    win_d = dram("w_in", [DEPTH, D, DIN])
    wout_d = dram("w_out", [DEPTH, D, D])
    gav_d = dram("gav", [DEPTH, 256])
    wsT_d = dram("wsT", [DEPTH, 128, 4, 128])
    bsT_d = dram("bsT", [128, DEPTH, 4])
    convw_d = dram("convw", [96, DEPTH, 8, 4])
    convb_d = dram("convb", [96, DEPTH, 8])
    gbias_d = dram("gbias", [DEPTH, 8])
    hnorm_d = dram("hnorm", [DEPTH, 384])
    qnorm_d = dram("qnorm", [DEPTH, 64])
    knorm_d = dram("knorm", [DEPTH, 64])
    consts_d = dram("consts", [128, 8, 128])
    y_d = dram("y", [NSEQ, S, D], kind="ExternalOutput")
    Bwinbf = [[Buf("winbf%d_%d" % (l, i)) for i in range(16)] for l in range(DEPTH)]
    Bwoutbf = [[Buf("woutbf%d_%d" % (l, i)) for i in range(8)] for l in range(DEPTH)]
    if debug:
        dbg_d = dram("dbg", [128, 8, GW], kind="ExternalOutput")

    def sb(name, shape, dt=F32):
        return nc.alloc_sbuf_tensor("sb_" + name, list(shape), dt)

    cF = sb("cF", [128, 4, 128]); BcF = Buf("cF")
    cB = sb("cB", [128, 4, 128], BF16); BcB = Buf("cB")
    identF, triU, pmaskI, onesF = cF[:, 0, :], cF[:, 1, :], cF[:, 2, :], cF[:, 3, :]
    identB, nTriL, nmaskS, nOnes = cB[:, 0, :], cB[:, 1, :], cB[:, 2, :], cB[:, 3, :]
    modT = sb("modT", [128, DEPTH, 24, NSEQ]); BmodT = Buf("modT")
    Amod = sb("Amod", [128, DEPTH, NSEQ, 8]); BAmod = Buf("Amod")
    normgT = sb("normgT", [128, DEPTH, 8]); Bnormg = Buf("normg")
    badaT = sb("badaT", [128, DEPTH, 24]); Bbada = Buf("bada")
    cact = sb("cact", [128, 8, NSEQ]); Bcact = Buf("cact")
    ctmp = sb("ctmp", [128, 8, NSEQ]); Bctmp = Buf("ctmp")

    sc.dma("sp", "c0", cF[:], consts_d[:, 0:4, :], writes=[BcF])
    sc.dma("sp", "c1", normgT[:], normgT_d[:, :, :], writes=[Bnormg])
    sc.dma("sp", "c2", badaT[:], badaT_d[:, :, :], writes=[Bbada])
    sc.dma("sp", "c3", cact[:], cT_d[:, :, :], writes=[Bcact])
    sc.op("act", lambda: A.activation(out=ctmp[:], in_=cact[:], func=AF.Tanh, scale=0.5), [Bcact], [Bctmp])
    sc.op("dve", lambda: V.scalar_tensor_tensor(out=ctmp[:], in0=ctmp[:], scalar=1.0, in1=cact[:], op0=ALU.add, op1=ALU.mult), [Bctmp, Bcact], [Bctmp])
    sc.op("dve", lambda: V.tensor_scalar_mul(out=cact[:], in0=ctmp[:], scalar1=0.5), [Bctmp], [Bcact])

    PB = [nc.alloc_psum_tensor("pb%d" % i, [128, 512], F32) for i in range(8)]
    PBb = [Buf("pb%d" % i, excl=True) for i in range(8)]
    PB4b = PB[4][:].bitcast(BF16)

    def pq(bank, c0, c1):
        return [PBb[bank]]

    xres = sb("xres", [128, NT, D]); Bx = [Buf("x%d" % t) for t in range(NT)]
    win = sb("win", [128, 8, DIN], BF16); Bwin = [Buf("win%d" % i) for i in range(8)]

    def load_win(l_):
        for kc_ in range(8):
            sc.dma("pool", "win", win[:, kc_, :], win_d[l_, kc_ * 128:(kc_ + 1) * 128, :], writes=[Bwin[kc_]])

    for t in range(NT):
        sc.dma("sp", "x%d" % t, xres[:, t, :], x_d[0, t * 128:(t + 1) * 128, :], writes=[Bx[t]])
    load_win(0)

    NSLOT = 3
    stg_cm = [nc.sbuf_tensor("stg%d" % i, [128, 8, 512], F32) for i in range(NSLOT)]
    stg = [cm.__enter__() for cm in stg_cm]
    Bstg = [Buf("stg%d" % i) for i in range(NSLOT)]
    cst_v = stg[0][:, 0:4, 0:128]
    sc.dma("sp", "stg0", cst_v, consts_d[:, 4:8, :], writes=[Bstg[0]])
    sc.op("dve", lambda: V.tensor_copy(out=cB[:], in_=cst_v), [Bstg[0]], [BcB])
    slot = 1
    for l in range(DEPTH):
        for pc in range(6):
            s = slot
            slot = (slot + 1) % NSLOT
            sv = stg[s]
            sc.dma("sp", "stg%d" % s, sv[:], wada_d[l, :, pc * 512:(pc + 1) * 512].rearrange("(k p) n -> p k n", p=128), writes=[Bstg[s]])
            for fc in range(4):
                j = pc * 4 + fc
                bank = j % 2
                for kc in range(8):
                    sc.op("pe", lambda: T.matmul(PB[bank][:, 0:NSEQ], lhsT=sv[:, kc, fc * 128:(fc + 1) * 128], rhs=cact[:, kc, :], start=(kc == 0), stop=(kc == 7)),
                          [Bstg[s], Bcact], pq(bank, 0, NSEQ))
                sc.op("dve", lambda: V.tensor_scalar_add(out=modT[:, l, j, :], in0=PB[bank][:, 0:NSEQ], scalar1=badaT[:, l, j:j + 1]),
                      pq(bank, 0, NSEQ) + [Bbada], [BmodT])
    for l in range(DEPTH):
        for b in range(NSEQ):
            sc.op("dve", lambda: V.scalar_tensor_tensor(out=Amod[:, l, b, :], in0=modT[:, l, 8:16, b], scalar=1.0, in1=normgT[:, l, :], op0=ALU.add, op1=ALU.mult),
                  [BmodT, Bnormg], [BAmod])
    sc.barrier()
    if STOP == "pro":
        return nc
    for cm in stg_cm[::-1]:
        cm.__exit__(None, None, None)

    wout = sb("wout", [128, 8, D], BF16); Bwout = Buf("wout")
    kT = sb("kT", [128, 3, S], BF16); BkT = [Buf("kT%d" % t) for t in range(NT)]
    vc = sb("vc", [128, NT, 384], BF16); Bvc = [Buf("vc%d" % t) for t in range(NT)]
    wsTb = sb("wsTb", [128, 4, 128], BF16); BwsTb = Buf("wsTb")
    gainA = sb("gainA", [128, 256]); BgainA = Buf("gainA")
    bsTs = sb("bsTs", [128, 4]); BbsT = Buf("bsT")
    convw = sb("convw", [96, 8, 4]); Bconvw = Buf("convw")
    convb = sb("convb", [96, 8]); Bconvb = Buf("convb")
    gbias = sb("gbias", [128, 8]); Bgbias = Buf("gbias")
    hnorm = sb("hnorm", [128, 384]); Bhnorm = Buf("hnorm")
    qnb = sb("qnb", [128, 64]); Bqnb = Buf("qnb")
    knb = sb("knb", [128, 64]); Bknb = Buf("knb")
    hT = sb("hT", [128, 8, GW], BF16); BhT = [Buf("hT%d" % r) for r in range(GT)]
    qkT = sb("qkT", [96, 8, GW], BF16); BqkT = [Buf("qkT%d" % u) for u in range(8)]
    hist = sb("hist", [96, 8, 3]); Bhist = [Buf("hist%d" % u) for u in range(8)]
    qTa = sb("qTa", [128, 3, GW], BF16); qTb = sb("qTb", [128, 3, GW], BF16)
    BqT = [Buf("qT%d" % r) for r in range(GT)]
    szc = sb("szc", [128, GT, 384], BF16); Bszc = [Buf("szc%d" % r) for r in range(GT)]
    ss = sb("ss", [128, 16]); Bss = Buf("ss")
    vaug = sb("vaug", [128, 4, 97], BF16); Bvaug = Buf("vaug")
    gt = sb("gt", [128, 8]); Bgt = Buf("gt")
    sm = sb("sm", [128, 48]); Bsm = Buf("sm")
    C32 = sb("C32", [96, 4, 97]); BC32 = [Buf("C32_%d" % h) for h in range(4)]
    Cb = sb("Cb", [96, 4, 97], BF16); BCb = [Buf("Cb_%d" % h) for h in range(4)]
    ARENA = 11328
    CH = 64
    arena = sb("arena", [128, ARENA // 4])
    arena_bf = arena[:].bitcast(BF16)
    Bar = [Buf("ar%d" % i) for i in range((ARENA + CH - 1) // CH)]

    def av(off, shape, dt=F32, parts=128):
        n = int(np.prod(shape[1:])) * (4 if dt == F32 else 2)
        assert off % 64 == 0 and off + n <= ARENA
        bufs = Bar[off // CH:(off + n + CH - 1) // CH]
        if dt == F32:
            ap = arena[0:parts, off // 4:(off + n) // 4]
        else:
            ap = arena_bf[0:parts, off // 2:(off + n) // 2]
        if len(shape) == 3:
            ap = ap.rearrange("p (a b) -> p a b", a=shape[1])
        return ap, bufs

    xn, Bxn = av(0, [128, D])
    raw0, Braw0 = av(0, [96, GW + 3], parts=96)
    raw1, Braw1 = av(1088, [96, GW + 3], parts=96)
    raw = [raw0, raw1]; Braw = [Braw0, Braw1]
    accs = [None] * 2; Baccs = [None] * 2; thqs = [None] * 2; Bthqs = [None] * 2
    for i_ in range(2):
        accs[i_], Baccs[i_] = av(2176 + i_ * 2048, [96, GW], parts=96)
        thqs[i_], Bthqs[i_] = av(3200 + i_ * 2048, [96, GW], parts=96)
    wsTf, BwsTf = av(0, [128, 4, 128])
    guv_, Bguv_ = av(0, [128, 512]); zgA_, BzgA_ = av(2048, [128, 256]); tmpA_, BtmpA_ = av(3072, [128, 256])
    w512 = [guv_, zgA_, tmpA_]; Bw512 = [Bguv_, BzgA_, BtmpA_]
    w384 = [None] * 3; Bw384 = [None] * 3
    for i_, o_ in enumerate((4096, 5632, 7168)):
        w384[i_], Bw384[i_] = av(o_, [128, 384])
    sq0, Bsq0 = av(0, [128, 384])
    sq97, Bsq97 = av(0, [128, 4, 97])
    PTm4, BPTm4 = av(0, [128, 4, 128], BF16)
    vp, Bvp = av(1024, [128, 4, 97], BF16)
    ktok, Bktok = av(1856, [128, 384], BF16)
    nd, Bnd = av(7168, [128, 4, 97])
    vn, Bvn = av(8768, [128, 256], BF16)
    ya, Bya = av(9280, [128, 256], BF16)
    yb, Byb = av(9792, [128, 384], BF16)
    qn16, Bqn16 = av(10560, [128, 384], BF16)
    kn16, Bkn16 = av(1536, [128, 384], BF16)
    ebuf = [None] * 4; Be = [None] * 4
    xbuf = [None] * 2; Bxb = [None] * 2; spb = [None] * 2; Bsp = [None] * 2; aTb = [None] * 2; BaT = [None] * 2; Lsum = [None] * 2; BLs = [None] * 2
    for i_ in range(4):
        ebuf[i_], Be[i_] = av(i_ * 1024, [128, GW])
    for i_ in range(2):
        xbuf[i_], Bxb[i_] = av(4096 + i_ * 1024, [128, GW])
        spb[i_], Bsp[i_] = av(6144 + i_ * 512, [128, GW], BF16)
        aTb[i_], BaT[i_] = av(7168 + i_ * 512, [128, GW], BF16)
        Lsum[i_], BLs[i_] = av(8192 + i_ * 512, [128, GW], BF16)
    yc, Byc_all = av(9216, [128, GT, 384], BF16)
    Byc = [[Byc_all for h in range(6)] for r in range(GT)]
    gate_bc, Bgate = av(4096, [128, D])
    gcol, Bgcol = av(8192, [128, 128])
    tmp5 = [None] * 2; Btmp5 = [None] * 2
    for i_ in range(2):
        tmp5[i_], Btmp5[i_] = av(i_ * 2048, [128, 512])
    ycT = hT
    print("sbuf bytes remaining", nc.sbuf_bytes_remaining)

    sc.op("pool", lambda: G.memset(vaug[:], 1.0), [], [Bvaug])
    sc.op("pool", lambda: G.memset(qTa[:], 0.0), [], BqT)
    sc.op("pool", lambda: G.memset(qTb[:], 0.0), [], BqT)

    def proj_tok(r, c0, c1, bank):
        n = c1 - c0
        for kc in range(8):
            sc.op("pe", lambda: T.matmul(PB[bank][:, 0:n], lhsT=hT[:, kc, r * 128:(r + 1) * 128], rhs=win[:, kc, c0:c1], start=(kc == 0), stop=(kc == 7)),
                  [BhT[r], Bwin], pq(bank, 0, n))

    def rstd_small(src_ap, dst_ap, n_inv, reads, writes):
        sc.op("act", lambda: A.activation(out=dst_ap, in_=src_ap, func=AF.Ln, scale=n_inv, bias=EPS), reads, writes)
        sc.op("act", lambda: A.activation(out=dst_ap, in_=dst_ap, func=AF.Exp, scale=-0.5), writes, writes)

    PBbf = {4: PB[4][:].bitcast(BF16), 2: PB[2][:].bitcast(BF16)}
    TRB = [4, 2]
    trn = [0]

    def tr_batch(srcs, reads):
        bank = TRB[trn[0] % 2]
        trn[0] += 1
        off = 0
        for (src, pin, cin) in srcs:
            o_ = off
            sc.op("pe", lambda: T.transpose(PBbf[bank][0:cin, o_:o_ + pin], src, identB[0:pin, 0:pin]), list(reads) + [BcB], pq(bank, 0, 1), pe_inline=True)
            off += pin
        return bank, PBbf[bank]

    tslot = [0]

    def next_tslot():
        tslot[0] = (tslot[0] + 1) % 8
        return tslot[0]

    def stop_at(tag):
        if STOP == tag:
            raise _Stop()

    win_loaded = {(0, 0): True}
    order = [(b_, l_) for b_ in range(NSEQ) for l_ in range(DEPTH)]

    try:
        for b in range(NSEQ):
          for l in range(DEPTH):
              last = (l == DEPTH - 1)
              if not win_loaded.get((b, l)):
                  load_win(l)
              sc.dma("pool", "wout", wout[:], wout_d[l].rearrange("(k p) n -> p k n", p=128), writes=[Bwout])
              sc.dma("sp", "p0", wsTf[:], wsT_d[l], writes=[BwsTf])
              sc.dma("sp", "p1", gainA[:], gav_d[l:l + 1, :].partition_broadcast(128), writes=[BgainA])
              sc.dma("sp", "p2", bsTs[:], bsT_d[:, l, :], writes=[BbsT])
              sc.dma("sp", "p3", convw[:], convw_d[:, l, :, :], writes=[Bconvw])
              sc.dma("sp", "p4", convb[:], convb_d[:, l, :], writes=[Bconvb])
              sc.dma("sp", "p5", gbias[:], gbias_d[l:l + 1, :].partition_broadcast(128), writes=[Bgbias])
              sc.dma("sp", "p6", hnorm[:], hnorm_d[l:l + 1, :].partition_broadcast(128), writes=[Bhnorm])
              sc.dma("sp", "p7", qnb[:], qnorm_d[l:l + 1, :].partition_broadcast(128), writes=[Bqnb])
              sc.dma("sp", "p8", knb[:], knorm_d[l:l + 1, :].partition_broadcast(128), writes=[Bknb])
              for g in range(4):
                  sc.op("dve", lambda: V.tensor_tensor(out=wsTb[:, g, :], in0=wsTf[:, g, :], in1=triU, op=ALU.mult), [BwsTf, BcF], [BwsTb])
              sc.op("dve", lambda: V.tensor_scalar_mul(out=gainA[:], in0=gainA[:], scalar1=0.5), [BgainA], [BgainA])
              sc.op("dve", lambda: V.tensor_scalar_mul(out=bsTs[:], in0=bsTs[:], scalar1=0.5), [BbsT], [BbsT])
              sc.op("dve", lambda: V.tensor_scalar_mul(out=hnorm[:], in0=hnorm[:], scalar1=0.25), [Bhnorm], [Bhnorm])
              sc.op("dve", lambda: V.tensor_scalar_mul(out=qnb[:], in0=qnb[:], scalar1=0.125), [Bqnb], [Bqnb])
              for c in range(8):
                  bank = c // 4
                  cs = (c % 4) * 128
                  sc.op("dve", lambda: V.tensor_scalar_mul(out=gcol[:], in0=onesF, scalar1=modT[:, l, 16 + c, b:b + 1]), [BcF, BmodT], [Bgcol])
                  sc.op("pe", lambda: T.matmul(PB[bank][:, cs:cs + 128], lhsT=gcol[:], rhs=identF, start=True, stop=True), [Bgcol, BcF], pq(bank, cs, cs + 128))
                  sc.op("act", lambda: A.copy(out=gate_bc[:, c * 128:(c + 1) * 128], in_=PB[bank][:, cs:cs + 128]), pq(bank, cs, cs + 128), [Bgate])
              for kc in range(8):
                  sc.op("pool", lambda: G.tensor_tensor(out=wout[:, kc, :], in0=wout[:, kc, :], in1=gate_bc[:], op=ALU.mult), [Bwout, Bgate], [Bwout])
              sc.op("pool", lambda: G.memset(C32[:], 0.0), [], BC32)
              sc.op("pool", lambda: G.memset(Cb[:], 0.0), [], BCb)
              sc.op("pool", lambda: G.memset(hist[:], 0.0), [], Bhist)

              stop_at('g0')
              for I in range(NG):
                  tiles = [I * GT + r for r in range(GT)]
                  for r, t in enumerate(tiles):
                      sc.op("dve", lambda: V.memset(ss[:, 0:1], 0.0), [], [Bss])
                      sc.op("act", lambda: A.activation(out=xn[:], in_=xres[:, t, :], func=AF.Square, accum_out=ss[:, 0:1]), [Bx[t], Bss], [Bxn, Bss], noinline=True)
                      stop_at('g1a')
                      rstd_small(ss[:, 0:1], ss[:, 1:2], 1.0 / D, [Bss], [Bss])
                      stop_at('g1b')
                      sc.op("dve", lambda: V.tensor_scalar_mul(out=xn[:], in0=xres[:, t, :], scalar1=ss[:, 1:2]), [Bx[t], Bss], [Bxn])
                      stop_at('g1c')
                      for c in range(8):
                          bank = 2 + c % 2
                          cs = (c // 2) * 128
                          sc.op("pe", lambda: T.transpose(PB[bank][:, cs:cs + 128], xn[:, c * 128:(c + 1) * 128], identF), [Bxn, BcF], pq(bank, cs, cs + 128), pe_inline=True)
                          stop_at('g1d')
                          if c % 2 == 0:
                              sc.op("dve", lambda: V.tensor_scalar(out=hT[:, c, r * 128:(r + 1) * 128], in0=PB[bank][:, cs:cs + 128],
                                                                   scalar1=Amod[:, l, b, c:c + 1], scalar2=modT[:, l, c, b:b + 1], op0=ALU.mult, op1=ALU.add),
                                    pq(bank, cs, cs + 128) + [BAmod, BmodT], [BhT[r]])
                              stop_at('g1e')
                          else:
                              sc.op("act", lambda: A.activation(out=hT[:, c, r * 128:(r + 1) * 128], in_=PB[bank][:, cs:cs + 128], func=AF.Identity,
                                                                scale=Amod[:, l, b, c:c + 1], bias=modT[:, l, c, b:b + 1]),
                                    pq(bank, cs, cs + 128) + [BAmod, BmodT], [BhT[r]])
                              if (r, c) == tuple(int(v) for v in os.environ.get("KRC", "0,1").split(",")):
                                  stop_at('g1f')
                  stop_at('g1')
                  def unit_front(u):
                      col0 = 768 + u * 96
                      rb = u % 2
                      ub = 2 + u % 2
                      acc, Bacc = accs[rb], Baccs[rb]
                      for kc in range(8):
                          sc.op("pe", lambda: T.matmul(PB[ub][0:96, 0:GW], lhsT=win[:, kc, col0:col0 + 96], rhs=hT[:, kc, :], start=(kc == 0), stop=(kc == 7)),
                                BhT + [Bwin], pq(ub, 0, GW))
                      sc.op("pool", lambda: G.tensor_copy(out=raw[rb][:, 0:3], in_=hist[:, u, :]), [Bhist[u]], [Braw[rb]])
                      sc.op("act", lambda: A.copy(out=raw[rb][:, 3:3 + GW], in_=PB[ub][0:96, 0:GW]), pq(ub, 0, GW), [Braw[rb]])
                      sc.op("pool", lambda: G.tensor_copy(out=hist[:, u, :], in_=raw[rb][:, GW:GW + 3]), [Braw[rb]], [Bhist[u]])
                      sc.op("act", lambda: A.activation(out=acc[:], in_=raw[rb][:, 3:3 + GW], func=AF.Identity, scale=convw[:, u, 3:4], bias=convb[:, u:u + 1]),
                            [Braw[rb], Bconvw, Bconvb], [Bacc])

                  def unit_back(u):
                      rb = u % 2
                      acc, Bacc, thq, Bthq = accs[rb], Baccs[rb], thqs[rb], Bthqs[rb]
                      for tap in (2, 1, 0):
                          sc.op("dve", lambda: V.scalar_tensor_tensor(out=acc[:], in0=raw[rb][:, tap:tap + GW], scalar=convw[:, u, tap:tap + 1], in1=acc[:], op0=ALU.mult, op1=ALU.add),
                                [Braw[rb], Bconvw, Bacc], [Bacc])
                      sc.op("act", lambda: A.activation(out=thq[:], in_=acc[:], func=AF.Tanh, scale=0.5), [Bacc], [Bthq])
                      sc.op("dve", lambda: V.scalar_tensor_tensor(out=qkT[:, u, :], in0=thq[:], scalar=1.0, in1=acc[:], op0=ALU.add, op1=ALU.mult), [Bthq, Bacc], [BqkT[u]])

                  unit_front(0)
                  for u in range(8):
                      if u + 1 < 8:
                          unit_front(u + 1)
                      unit_back(u)
                  stop_at('g2')
                  for r, t in enumerate(tiles):
                      rc = slice(r * 128, (r + 1) * 128)
                      guv, zgA, tmpA = w512[0], w512[1], w512[2]
                      proj_tok(r, 0, 512, 0)
                      sc.op("act", lambda: A.activation(out=guv[:], in_=PB[0][:, 0:512], func=AF.Gelu_apprx_tanh), pq(0, 0, 512), [Bw512[0]])
                      proj_tok(r, 512, 768, 1)
                      sc.op("act", lambda: A.activation(out=tmpA[:, 0:256], in_=PB[1][:, 0:256], func=AF.Tanh, scale=0.5), pq(1, 0, 256), [Bw512[2]])
                      sc.op("dve", lambda: V.scalar_tensor_tensor(out=zgA[:, 0:256], in0=tmpA[:, 0:256], scalar=1.0, in1=PB[1][:, 0:256], op0=ALU.add, op1=ALU.mult),
                            [Bw512[2]] + pq(1, 0, 256), [Bw512[1]])
                      sc.op("dve", lambda: V.tensor_tensor(out=zgA[:, 0:256], in0=zgA[:, 0:256], in1=guv[:, 0:256], op=ALU.mult), [Bw512[1], Bw512[0]], [Bw512[1]])
                      to_, ozg, tzb = w384[0], w384[1], w384[2]
                      proj_tok(r, 1920, 2304, 0)
                      sc.op("act", lambda: A.activation(out=to_[:], in_=PB[0][:, 0:384], func=AF.Tanh, scale=0.5), pq(0, 0, 384), [Bw384[0]])
                      proj_tok(r, 2304, 2696, 1)
                      sc.op("act", lambda: A.activation(out=tzb[:], in_=PB[1][:, 0:384], func=AF.Tanh, scale=0.5), pq(1, 0, 384), [Bw384[2]])
                      sc.op("dve", lambda: V.scalar_tensor_tensor(out=ozg[:], in0=tzb[:], scalar=1.0, in1=PB[1][:, 0:384], op0=ALU.add, op1=ALU.mult),
                            [Bw384[2]] + pq(1, 0, 384), [Bw384[1]])
                      sc.op("dve", lambda: V.scalar_tensor_tensor(out=ozg[:], in0=to_[:], scalar=1.0, in1=ozg[:], op0=ALU.add, op1=ALU.mult), [Bw384[0], Bw384[1]], [Bw384[1]])
                      sc.op("pool", lambda: G.tensor_tensor(out=ozg[:], in0=ozg[:], in1=hnorm[:], op=ALU.mult), [Bw384[1], Bhnorm], [Bw384[1]])
                      sc.op("dve", lambda: V.tensor_tensor(out=gt[:], in0=PB[1][:, 384:392], in1=gbias[:], op=ALU.add), pq(1, 384, 392) + [Bgbias], [Bgt])
                      proj_tok(r, 3848, 4232, 0)
                      sc.op("act", lambda: A.activation(out=to_[:], in_=PB[0][:, 0:384], func=AF.Tanh, scale=0.5), pq(0, 0, 384), [Bw384[0]])
                      sc.op("dve", lambda: V.scalar_tensor_tensor(out=szc[:, r, :], in0=to_[:], scalar=1.0, in1=PB[0][:, 0:384], op0=ALU.add, op1=ALU.mult),
                            [Bw384[0]] + pq(0, 0, 384), [Bszc[r]])
                      proj_tok(r, 1536, 1920, 1)
                      sc.op("act", lambda: A.copy(out=vaug[:, :, 0:96], in_=PB[1][:, 0:384].rearrange("p (h d) -> p h d", h=4)), pq(1, 0, 384), [Bvaug])
                      proj_tok(r, 3464, 3848, 0)
                      sc.op("act", lambda: A.copy(out=vc[:, t, :], in_=PB[0][:, 0:384]), pq(0, 0, 384), [Bvc[t]])
                      proj_tok(r, 2696, 3080, 1)
                      proj_tok(r, 3080, 3464, 0)
                      sc.op("dve", lambda: V.tensor_tensor(out=tmpA[:, 0:256], in0=guv[:, 256:512], in1=guv[:, 256:512], op=ALU.mult), [Bw512[0]], [Bw512[2]])
                      sc.op("dve", lambda: V.reduce_sum(out=ss[:, 2:6], in_=tmpA[:, 0:256].rearrange("p (g d) -> p g d", g=4), axis=AX.X), [Bw512[2]], [Bss])
                      rstd_small(ss[:, 2:6], ss[:, 2:6], 1.0 / 64, [Bss], [Bss])
                      sc.op("dve", lambda: V.tensor_tensor(out=tmpA[:, 0:256].rearrange("p (g d) -> p g d", g=4), in0=guv[:, 256:512].rearrange("p (g d) -> p g d", g=4),
                                                           in1=ss[:, 2:6].unsqueeze(2).to_broadcast([128, 4, 64]), op=ALU.mult), [Bw512[0], Bss], [Bw512[2]])
                      sc.op("dve", lambda: V.tensor_tensor(out=vn[:], in0=tmpA[:, 0:256], in1=gainA[:], op=ALU.mult), [Bw512[2], BgainA], [Bvn])
                      for g in range(4):
                          sc.op("pe", lambda: T.matmul(PB[7][:, g * 64:(g + 1) * 64], lhsT=wsTb[:, g, :], rhs=vn[:, g * 64:(g + 1) * 64], start=True, stop=True),
                                [BwsTb, Bvn], pq(7, g * 64, g * 64 + 64))
                      for g in range(4):
                          sc.op("dve", lambda: V.scalar_tensor_tensor(out=ya[:, g * 64:(g + 1) * 64], in0=PB[7][:, g * 64:(g + 1) * 64], scalar=bsTs[:, g:g + 1],
                                                                      in1=zgA[:, g * 64:(g + 1) * 64], op0=ALU.add, op1=ALU.mult),
                                pq(7, g * 64, g * 64 + 64) + [BbsT, Bw512[1]], [Bya])
                      sc.op("act", lambda: A.activation(out=sm[:, 0:4], in_=gt[:, 4:8], func=AF.Exp, scale=-1.0), [Bgt], [Bsm])
                      sc.op("act", lambda: A.activation(out=sm[:, 0:4], in_=sm[:, 0:4], func=AF.Ln, bias=1.0), [Bsm], [Bsm])
                      sc.op("pe", lambda: T.matmul(PB[6][:, 400:404], lhsT=triU, rhs=sm[:, 0:4], start=True, stop=True), [BcF, Bsm], pq(6, 400, 404), pe_inline=True)
                      sc.op("pe", lambda: T.matmul(PB[6][:, 404:408], lhsT=onesF, rhs=sm[:, 0:4], start=True, stop=True), [BcF, Bsm], pq(6, 404, 408))
                      sc.op("dve", lambda: V.tensor_copy(out=sm[:, 8:16], in_=PB[6][:, 400:408]), pq(6, 400, 408), [Bsm])
                      sc.op("dve", lambda: V.tensor_tensor(out=sm[:, 16:20], in0=gt[:, 0:4], in1=sm[:, 8:12], op=ALU.add), [Bgt, Bsm], [Bsm])
                      sc.op("act", lambda: A.activation(out=sm[:, 16:20], in_=sm[:, 16:20], func=AF.Exp), [Bsm], [Bsm])
                      sc.op("act", lambda: A.activation(out=sm[:, 24:32], in_=sm[:, 8:16], func=AF.Exp, scale=-1.0), [Bsm], [Bsm])
                      sc.op("dve", lambda: V.tensor_tensor(out=vp[:], in0=vaug[:], in1=sm[:, 16:20].unsqueeze(2).to_broadcast([128, 4, 97]), op=ALU.mult), [Bvaug, Bsm], [Bvp])
                      for h in range(4):
                          sc.op("pe", lambda: T.matmul(PB[3][:, h * 128:(h + 1) * 128], lhsT=qkT[:, 4 + h, rc], rhs=qkT[:, h, rc], start=True, stop=True),
                                [BqkT[4 + h], BqkT[h]], pq(3, 0, 512))
                      sc.op("dve", lambda: V.tensor_tensor(out=PTm4[:], in0=PB[3][:, 0:512].rearrange("p (h t) -> p h t", h=4),
                                                           in1=triU.unsqueeze(1).to_broadcast([128, 4, 128]), op=ALU.mult), pq(3, 0, 512) + [BcF], [BPTm4])
                      for h in range(4):
                          hv = slice(h * 97, (h + 1) * 97)
                          sc.op("pe", lambda: T.matmul(PB[5][:, hv], lhsT=PTm4[:, h, :], rhs=vp[:, h, :], start=True, stop=False), [BPTm4, Bvp], pq(5, 0, 388))
                          sc.op("pe", lambda: T.matmul(PB[5][:, hv], lhsT=qkT[:, h, rc], rhs=Cb[:, h, :], start=False, stop=True), [BqkT[h]] + BCb, pq(5, 0, 388))
                      kb, kps = tr_batch([(qkT[:, 4 + h, rc], 96, 128) for h in range(4)], [BqkT[4 + h] for h in range(4)])
                      sc.op("act", lambda: A.copy(out=ktok[:], in_=kps[:, 0:384]), pq(kb, 0, 1), [Bktok])
                      for h in range(4):
                          hv = slice(h * 97, (h + 1) * 97)
                          sc.op("pe", lambda: T.matmul(PB[7][0:96, hv], lhsT=ktok[:, h * 96:(h + 1) * 96], rhs=vp[:, h, :], start=True, stop=True), [Bktok, Bvp], pq(7, 0, 388))
                      sc.op("dve", lambda: V.tensor_tensor(out=nd[:], in0=PB[5][:, 0:388].rearrange("p (h d) -> p h d", h=4),
                                                           in1=sm[:, 24:28].unsqueeze(2).to_broadcast([128, 4, 97]), op=ALU.mult), pq(5, 0, 388) + [Bsm], [Bnd])
                      sc.op("dve", lambda: V.scalar_tensor_tensor(out=sm[:, 32:36], in0=nd[:, :, 96], scalar=-1.0, in1=nd[:, :, 96], op0=ALU.mult, op1=ALU.max), [Bnd], [Bsm])
                      sc.op("dve", lambda: V.tensor_scalar_max(out=sm[:, 32:36], in0=sm[:, 32:36], scalar1=CS_INV), [Bsm], [Bsm])
                      sc.op("dve", lambda: V.reciprocal(out=sm[:, 32:36], in_=sm[:, 32:36]), [Bsm], [Bsm])
                      sc.op("dve", lambda: V.tensor_tensor(out=sq97[:], in0=nd[:], in1=nd[:], op=ALU.mult), [Bnd], [Bsq97])
                      sc.op("dve", lambda: V.reduce_sum(out=sm[:, 36:40], in_=sq97[:, :, 0:96], axis=AX.X), [Bsq97], [Bsm])
                      sc.op("dve", lambda: V.tensor_tensor(out=sm[:, 40:44], in0=sm[:, 32:36], in1=sm[:, 32:36], op=ALU.mult), [Bsm], [Bsm])
                      sc.op("dve", lambda: V.tensor_tensor(out=sm[:, 40:44], in0=sm[:, 40:44], in1=sm[:, 36:40], op=ALU.mult), [Bsm], [Bsm])
                      rstd_small(sm[:, 40:44], sm[:, 40:44], 1.0 / 96, [Bsm], [Bsm])
                      sc.op("dve", lambda: V.tensor_tensor(out=sm[:, 44:48], in0=sm[:, 40:44], in1=sm[:, 32:36], op=ALU.mult), [Bsm], [Bsm])
                      for h in range(4):
                          sc.op("dve", lambda: V.scalar_tensor_tensor(out=yb[:, h * 96:(h + 1) * 96], in0=nd[:, h, 0:96], scalar=sm[:, 44 + h:45 + h], in1=ozg[:, h * 96:(h + 1) * 96],
                                                                      op0=ALU.mult, op1=ALU.mult), [Bnd, Bsm, Bw384[1]], [Byb])
                      sc.op("dve", lambda: V.tensor_tensor(out=C32[:], in0=C32[:], in1=PB[7][0:96, 0:388].rearrange("p (h d) -> p h d", h=4), op=ALU.add),
                            BC32 + pq(7, 0, 388), BC32)
                      sc.op("pool", lambda: G.tensor_tensor(out=C32[:], in0=C32[:], in1=sm[0:96, 28:32].unsqueeze(2).to_broadcast([96, 4, 97]), op=ALU.mult), BC32 + [Bsm], BC32)
                      sc.op("pool", lambda: G.tensor_copy(out=Cb[:], in_=C32[:]), BC32, BCb)
                      cbuf = [(1, sq0, Bsq0, w384[0], Bw384[0], qn16, Bqn16, qnb, Bqnb, 2), (0, w384[1], Bw384[1], w384[2], Bw384[2], kn16, Bkn16, knb, Bknb, 8)]
                      for (pb_, sqx, Bsqx, fx, Bfx, n16, Bn16, gsrc, Bg, so) in cbuf:
                          sc.op("act", lambda: A.activation(out=sqx[:], in_=PB[pb_][:, 0:384], func=AF.Square), pq(pb_, 0, 384), [Bsqx])
                      for (pb_, sqx, Bsqx, fx, Bfx, n16, Bn16, gsrc, Bg, so) in cbuf:
                          sc.op("dve", lambda: V.reduce_sum(out=ss[:, so:so + 6], in_=sqx[:].rearrange("p (h d) -> p h d", h=6), axis=AX.X), [Bsqx], [Bss])
                      rstd_small(ss[:, 2:14], ss[:, 2:14], 1.0 / 64, [Bss], [Bss])
                      for (pb_, sqx, Bsqx, fx, Bfx, n16, Bn16, gsrc, Bg, so) in cbuf:
                          sc.op("dve", lambda: V.tensor_tensor(out=fx[:].rearrange("p (h d) -> p h d", h=6), in0=PB[pb_][:, 0:384].rearrange("p (h d) -> p h d", h=6),
                                                               in1=ss[:, so:so + 6].unsqueeze(2).to_broadcast([128, 6, 64]), op=ALU.mult), pq(pb_, 0, 384) + [Bss], [Bfx])
                      for (pb_, sqx, Bsqx, fx, Bfx, n16, Bn16, gsrc, Bg, so) in cbuf:
                          sc.op("dve" if pb_ else "pool", (lambda: V.tensor_tensor(out=n16[:].rearrange("p (h d) -> p h d", h=6), in0=fx[:].rearrange("p (h d) -> p h d", h=6),
                                                                                    in1=gsrc[:].unsqueeze(1).to_broadcast([128, 6, 64]), op=ALU.mult)) if pb_ else
                                (lambda: G.tensor_tensor(out=n16[:].rearrange("p (h d) -> p h d", h=6), in0=fx[:].rearrange("p (h d) -> p h d", h=6),
                                                         in1=gsrc[:].unsqueeze(1).to_broadcast([128, 6, 64]), op=ALU.mult)), [Bfx, Bg], [Bn16])
                      tb, tps_ = tr_batch([(qn16[:, p * 128:(p + 1) * 128], 128, 128) for p in range(3)], [Bqn16])
                      sc.op("dve", lambda: V.tensor_copy(out=qTa[0:64, :, rc], in_=tps_[0:64, 0:384].rearrange("p (c t) -> p c t", c=3)), pq(tb, 0, 1), [BqT[r]])
                      sc.op("act", lambda: A.copy(out=qTb[64:128, :, rc], in_=tps_[64:128, 0:384].rearrange("p (c t) -> p c t", c=3)), pq(tb, 0, 1), [BqT[r]])
                      tb, tps_ = tr_batch([(kn16[:, p * 128:(p + 1) * 128], 128, 128) for p in range(3)], [Bkn16])
                      sc.op("act", lambda: A.copy(out=kT[:, :, t * 128:(t + 1) * 128], in_=tps_[:, 0:384].rearrange("p (c t) -> p c t", c=3)), pq(tb, 0, 1), [BkT[t]])
                      tb, tps_ = tr_batch([(ya[:, j * 128:(j + 1) * 128], 128, 128) for j in range(2)] + [(yb[:, j * 128:(j + 1) * 128], 128, 128) for j in range(3)], [Bya, Byb])
                      sc.op("dve", lambda: V.tensor_copy(out=ycT[:, 0:5, rc], in_=tps_[:, 0:640].rearrange("p (c t) -> p c t", c=5)), pq(tb, 0, 1), [BhT[r]])
                  stop_at('g3')
                  if I == NG - 1:
                      nxt = order.index((b, l)) + 1
                      if nxt < len(order):
                          load_win(order[nxt][1])
                          win_loaded[order[nxt]] = True
                  jmax = I * GT + GT - 1
                  its = []
                  for hp in range(3):
                      for j in range(jmax, -1, -1):
                          for e_ in range(2):
                              its.append((hp * 2 + e_, j))
                  ZB = [5, 3, 0]
                  CBk = [6, 2]
                  nit = len(its)

                  def geo(n):
                      hd, j = its[n]
                      r0 = max(0, j - I * GT)
                      return hd, j, hd // 2, hd % 2, r0, r0 * 128, GW - r0 * 128

                  for e_ in range(2):
                      sc.op("dve", lambda: V.memset(PB[7][:, e_ * 128:(e_ + 1) * 128], 0.0), [], pq(7, 0, 128))
                  for k in range(nit + 5):
                      if k < nit:
                          hd, j, p, e_, r0, t0, N = geo(k)
                          diag = j >= I * GT
                          qs = (qTa if e_ == 0 else qTb)[:, p, t0:GW]
                          zb = ZB[k % 3]
                          sc.op("pe", lambda: T.matmul(PB[zb][:, 0:N], lhsT=kT[:, p, j * 128:(j + 1) * 128], rhs=qs, start=True, stop=not diag), [BkT[j]] + BqT, pq(zb, 0, N))
                          if diag:
                              sc.op("pe", lambda: T.matmul(PB[zb][:, 0:128], lhsT=identB, rhs=nmaskS, start=False, stop=True), [BcB], pq(zb, 0, 128))
                      n = k - 1
                      if 0 <= n < nit:
                          hd, j, p, e_, r0, t0, N = geo(n)
                          zb = ZB[n % 3]
                          sc.op("act", lambda: A.activation(out=ebuf[n % 4][:, 0:N], in_=PB[zb][:, 0:N], func=AF.Exp), pq(zb, 0, N), [Be[n % 4]])
                      n = k - 2
                      if 0 <= n < nit:
                          hd, j, p, e_, r0, t0, N = geo(n)
                          cb = CBk[n % 2]
                          carry = j < jmax
                          sc.op("pe", lambda: T.matmul(PB[cb][:, 0:N], lhsT=nTriL, rhs=spb[n % 2][:, 0:N], start=True, stop=not carry), [BcB, Bsp[n % 2]], pq(cb, 0, N), pe_inline=True)
                          if carry:
                              sc.op("pe", lambda: T.matmul(PB[cb][:, 0:N], lhsT=nOnes, rhs=Lsum[e_][:, t0:GW], start=False, stop=True), [BcB, BLs[e_]], pq(cb, 0, N), pe_inline=True)
                          if j == jmax:
                              sc.op("pool", lambda: G.memset(Lsum[e_][:], 0.0), [], [BLs[e_]])
                          if j > 0:
                              sc.op("pool", lambda: G.tensor_tensor(out=Lsum[e_][:, t0:GW], in0=Lsum[e_][:, t0:GW], in1=spb[n % 2][:, 0:N], op=ALU.add), [BLs[e_], Bsp[n % 2]], [BLs[e_]])
                      n = k - 3
                      if 0 <= n < nit:
                          hd, j, p, e_, r0, t0, N = geo(n)
                          cb = CBk[n % 2]
                          sc.op("act", lambda: A.activation(out=xbuf[n % 2][:, 0:N], in_=PB[cb][:, 0:N], func=AF.Exp), pq(cb, 0, N), [Bxb[n % 2]])
                          sc.op("dve", lambda: V.tensor_tensor(out=aTb[n % 2][:, 0:N], in0=xbuf[n % 2][:, 0:N], in1=ebuf[n % 4][:, 0:N], op=ALU.mult), [Bxb[n % 2], Be[n % 4]], [BaT[n % 2]])
                      n = k - 1
                      if 0 <= n < nit:
                          hd, j, p, e_, r0, t0, N = geo(n)
                          sc.op("act", lambda: A.activation(out=spb[n % 2][:, 0:N], in_=ebuf[n % 4][:, 0:N], func=AF.Ln, bias=1.0), [Be[n % 4]], [Bsp[n % 2]])
                      n = k - 4
                      if 0 <= n < nit:
                          hd, j, p, e_, r0, t0, N = geo(n)
                          for rr in range(r0, GT):
                              oc = e_ * 128 + rr * 64
                              sc.op("pe", lambda: T.matmul(PB[7][:, oc:oc + 64], lhsT=aTb[n % 2][:, (rr - r0) * 128:(rr - r0 + 1) * 128], rhs=vc[:, j, hd * 64:(hd + 1) * 64],
                                                           start=False, stop=(j == 0), skip_group_check=True), [BaT[n % 2], Bvc[j]], pq(7, oc, oc + 64))
                          if j == 0:
                              for rr in range(GT):
                                  oc = e_ * 128 + rr * 64
                                  sc.op("dve", lambda: V.scalar_tensor_tensor(out=yc[:, rr, hd * 64:(hd + 1) * 64], in0=PB[7][:, oc:oc + 64], scalar=0.5,
                                                                              in1=szc[:, rr, hd * 64:(hd + 1) * 64], op0=ALU.mult, op1=ALU.mult),
                                        pq(7, oc, oc + 64) + [Bszc[rr]], [Byc[rr][hd]])
                              if hd < 4:
                                  sc.op("dve", lambda: V.memset(PB[7][:, e_ * 128:(e_ + 1) * 128], 0.0), [], pq(7, 0, 128))
                  stop_at('g4')
                  for r, t in enumerate(tiles):
                      rc = slice(r * 128, (r + 1) * 128)
                      tb, tps_ = tr_batch([(yc[:, r, j * 128:(j + 1) * 128], 128, 128) for j in range(3)], [Byc_all])
                      sc.op("act", lambda: A.copy(out=ycT[:, 5:8, rc], in_=tps_[:, 0:384].rearrange("p (c t) -> p c t", c=3)), pq(tb, 0, 1), [BhT[r]])
                      if debug and b == 0 and l == 0 and I == 0 and r == GT - 1:
                          for c in range(8):
                              sc.op("dve", lambda: V.tensor_copy(out=tmp5[0][:, 0:GW], in_=ycT[:, c, :]), BhT, [Btmp5[0]])
                              sc.dma("sp", "dbg", dbg_d[:, c, :], tmp5[0][:, 0:GW], reads=[Btmp5[0]])
                      for half in range(2):
                          bank = half
                          hs = slice(half * 512, (half + 1) * 512)
                          for kc in range(8):
                              sc.op("pe", lambda: T.matmul(PB[bank][:, 0:512], lhsT=ycT[:, kc, rc], rhs=wout[:, kc, hs], start=(kc == 0), stop=(kc == 7)),
                                    [BhT[r], Bwout], pq(bank, 0, 512))
                          sc.op("dve", lambda: V.tensor_tensor(out=xres[:, t, hs], in0=PB[bank][:, 0:512], in1=xres[:, t, hs], op=ALU.add), pq(bank, 0, 512) + [Bx[t]], [Bx[t]])
                      if last:
                          sc.dma("sp", "o%d" % t, y_d[b, t * 128:(t + 1) * 128, :], xres[:, t, :], reads=[Bx[t]])
                          if b + 1 < NSEQ:
                              sc.dma("sp", "x%d" % t, xres[:, t, :], x_d[b + 1, t * 128:(t + 1) * 128, :], writes=[Bx[t]])
    except _Stop:
        pass
    sc.barrier()
    print("instructions", sc.ninst, "waits", sc.nwaits)
    return nc


def host_inputs(inp, core, NSEQ, DEPTH):
    f = lambda a: np.ascontiguousarray(np.asarray(a, dtype=np.float32))
    b0 = core * NSEQ
    m = {}
    m["x"] = f(inp["x"][b0:b0 + NSEQ])
    c = np.asarray(inp["c"], np.float32)[b0:b0 + NSEQ]
    m["cT"] = f(c.reshape(NSEQ, 8, 128).transpose(2, 1, 0))
    m["w_ada"] = f(inp["w_ada"][:DEPTH])
    m["b_adaT"] = f(np.asarray(inp["b_ada"], np.float32)[:DEPTH].reshape(DEPTH, 24, 128).transpose(2, 0, 1))
    m["normgT"] = f(np.asarray(inp["norm_g"], np.float32)[:DEPTH].reshape(DEPTH, 8, 128).transpose(2, 0, 1))
    m["w_in"] = f(inp["w_in"][:DEPTH])
    m["w_out"] = f(inp["w_out"][:DEPTH])
    m["gav"] = f(np.asarray(inp["ga_v_norm"], np.float32)[:DEPTH].reshape(DEPTH, 256))
    m["wsT"] = f(np.asarray(inp["ga_ws"], np.float32)[:DEPTH].transpose(0, 3, 1, 2))
    m["bsT"] = f(np.asarray(inp["ga_bs"], np.float32)[:DEPTH].transpose(2, 0, 1))
    cw = np.asarray(inp["mb_conv_w"], np.float32)[:DEPTH]
    m["convw"] = f(cw.reshape(DEPTH, 4, 8, 96).transpose(3, 0, 2, 1))
    m["convb"] = f(np.asarray(inp["mb_conv_b"], np.float32)[:DEPTH].reshape(DEPTH, 8, 96).transpose(2, 0, 1))
    m["gbias"] = f(np.concatenate([np.asarray(inp["mb_b_i"], np.float32)[:DEPTH], np.asarray(inp["mb_b_f"], np.float32)[:DEPTH]], axis=1))
    m["hnorm"] = f(np.asarray(inp["mb_h_norm"], np.float32)[:DEPTH].reshape(DEPTH, 384))
    m["qnorm"] = f(np.asarray(inp["sc_q_norm"], np.float32)[:DEPTH])
    m["knorm"] = f(np.asarray(inp["sc_k_norm"], np.float32)[:DEPTH])
    i = np.arange(128)
    cs = np.zeros((128, 8, 128), np.float32)
    cs[:, 0] = np.eye(128)
    cs[:, 1] = (i[:, None] <= i[None, :])
    cs[:, 2] = np.where(i[:, None] <= i[None, :], 0.0, BIG)
    cs[:, 3] = 1.0
    cs[:, 4] = np.eye(128)
    cs[:, 5] = -(i[:, None] >= i[None, :]).astype(np.float32)
    cs[:, 6] = np.where(i[:, None] < i[None, :], 0.0, -BIG)
    cs[:, 7] = -1.0
    m["consts"] = cs
    return m


_NC_CACHE = {}


def run(inp, NSEQ, S, DEPTH, ncores, debug=False):
    key = (NSEQ, S, DEPTH, debug)
    if key not in _NC_CACHE:
        _NC_CACHE[key] = build(NSEQ, S, DEPTH, debug)
    nc = _NC_CACHE[key]
    in_maps = [host_inputs(inp, c, NSEQ, DEPTH) for c in range(ncores)]
    res = run_bass_kernel_spmd(nc, in_maps, core_ids=list(range(ncores)))
    out = np.concatenate([r["y"] for r in res.results], axis=0)
    if debug:
        return out, res.results[0]["dbg"]
    return out


def kernel(**inputs):
    out = run(inputs, 4, 2048, 2, 8)
    return np.ascontiguousarray(out.astype(np.float32))
```
